# Optimizing a Trainium2 kernel written in Bass

```python
import math
import jax, jax.numpy as jnp
from jax import lax
import numpy as np

D_MODEL = 2048
BATCH = 4
SEQ = 8192
DEPTH = 2

GRID_W = 64
CTX_LEN = 256
EPS = 1e-6
N_MOD = 6

POOL_WINDOWS = (2, 4, 8, 16)
N_POOL = len(POOL_WINDOWS)
POOL_DIM = D_MODEL // 2
POOL_GROUP = POOL_DIM // N_POOL

HEAD_DIM = 128
N_Q_HEADS = (D_MODEL // 2) // HEAD_DIM
N_KV_HEADS = N_Q_HEADS // 4
Q_PER_KV = N_Q_HEADS // N_KV_HEADS
Q_DIM = N_Q_HEADS * HEAD_DIM
KV_DIM = N_KV_HEADS * HEAD_DIM
ATTN_BLOCK = 128
ROPE_THETA = 10000.0
ROPE_PAIRS_PER_AXIS = HEAD_DIM // 4
AB_IN = POOL_DIM + Q_DIM + 2 * KV_DIM
AB_OUT = POOL_DIM + Q_DIM

DN_DK = 128
DN_DV = 128
DN_K_HEADS = D_MODEL // DN_DK
DN_V_HEADS = 2 * DN_K_HEADS
V_PER_K = DN_V_HEADS // DN_K_HEADS
DN_QK_DIM = DN_K_HEADS * DN_DK
DN_V_DIM = DN_V_HEADS * DN_DV
DN_QKV_DIM = 2 * DN_QK_DIM + DN_V_DIM
CONV_K = 5
DN_CHUNK = 64
C_IN = DN_QKV_DIM + DN_V_DIM + 4 * DN_V_HEADS

D_FF = 4 * D_MODEL

kernel_name = 'hybrid_pool_gqa_deltanet_diffusion_trunk'


def rmsnorm(x, g):
    xf = x.astype(jnp.float32)
    y = xf * lax.rsqrt(jnp.mean(xf * xf, axis=-1, keepdims=True) + EPS)
    return (y * g.astype(jnp.float32)).astype(x.dtype)


def modulate(h, shift, scale):
    return h * (1 + scale) + shift


def l2norm(x):
    return x * lax.rsqrt(jnp.sum(x * x, axis=-1, keepdims=True) + EPS)


def window_mean_minus_self(u, window):
    L = u.shape[1]
    uf = u.astype(jnp.float32)
    cs = jnp.concatenate([jnp.zeros_like(uf[:, :1]), jnp.cumsum(uf, axis=1)], axis=1)
    t = jnp.arange(L)
    lo = jnp.clip(t - window // 2, 0, L)
    hi = jnp.clip(t + window - window // 2, 0, L)
    mean = (cs[:, hi] - cs[:, lo]) / (hi - lo).astype(jnp.float32)[None, :, None]
    return (mean - uf).astype(u.dtype)


def pool_mix(u, pool_w, pool_scale):
    B, L = u.shape[:2]
    ug = u.reshape(B, L, N_POOL, POOL_GROUP)
    d = jnp.stack([window_mean_minus_self(ug[:, :, gi], w) for gi, w in enumerate(POOL_WINDOWS)], axis=2)
    y = jnp.einsum('blgc,gcd->blgd', d, pool_w).reshape(B, L, POOL_DIM)
    return y * pool_scale


def axial_rope_tables(num_tokens):
    rows = num_tokens // GRID_W
    row = jnp.repeat(jnp.arange(rows, dtype=jnp.float32), GRID_W)
    col = jnp.tile(jnp.arange(GRID_W, dtype=jnp.float32), rows)
    inv = ROPE_THETA ** (-jnp.arange(ROPE_PAIRS_PER_AXIS, dtype=jnp.float32) / ROPE_PAIRS_PER_AXIS)
    ang = jnp.concatenate([row[:, None] * inv, col[:, None] * inv], axis=-1)
    return jnp.cos(ang), jnp.sin(ang)


def apply_rope(x, cos, sin):
    half = HEAD_DIM // 2
    xf = x.astype(jnp.float32)
    x1, x2 = xf[..., :half], xf[..., half:]
    c = cos[None, :, None, :]
    s = sin[None, :, None, :]
    return jnp.concatenate([x1 * c - x2 * s, x1 * s + x2 * c], axis=-1).astype(x.dtype)


def gqa_attend(q, k, v):
    s = jnp.einsum('bqkgd,bskd->bkgqs', q, k, preferred_element_type=jnp.float32) * (HEAD_DIM ** -0.5)
    p = jax.nn.softmax(s, axis=-1).astype(v.dtype)
    return jnp.einsum('bkgqs,bskd->bqkgd', p, v)


def pool_attn_mixer(h_lat, h_ctx, w_in, pool_w, pool_scale, q_norm, k_norm, w_out, cos, sin, need_ctx):
    def split(h):
        B, L = h.shape[:2]
        p = h @ w_in
        u = p[..., :POOL_DIM]
        q = p[..., POOL_DIM:POOL_DIM + Q_DIM].reshape(B, L, N_Q_HEADS, HEAD_DIM)
        k = p[..., POOL_DIM + Q_DIM:POOL_DIM + Q_DIM + KV_DIM].reshape(B, L, N_KV_HEADS, HEAD_DIM)
        v = p[..., POOL_DIM + Q_DIM + KV_DIM:].reshape(B, L, N_KV_HEADS, HEAD_DIM)
        return u, q, rmsnorm(k, k_norm), v

    B, L = h_lat.shape[:2]
    Lc = h_ctx.shape[1]
    u_l, q_l, k_l, v_l = split(h_lat)
    u_c, q_c, k_c, v_c = split(h_ctx)
    q_l = apply_rope(rmsnorm(q_l, q_norm), cos, sin)
    k_l = apply_rope(k_l, cos, sin)
    k_all = jnp.concatenate([k_c, k_l], axis=1)
    v_all = jnp.concatenate([v_c, v_l], axis=1)
    n_blk = L // ATTN_BLOCK
    q_blocks = q_l.reshape(B, n_blk, ATTN_BLOCK, N_KV_HEADS, Q_PER_KV, HEAD_DIM).swapaxes(0, 1)
    o = lax.map(lambda qb: gqa_attend(qb, k_all, v_all), q_blocks)
    o_l = o.swapaxes(0, 1).reshape(B, L, Q_DIM)
    y_l = jnp.concatenate([pool_mix(u_l, pool_w, pool_scale), o_l], axis=-1) @ w_out
    if not need_ctx:
        return y_l, None
    q_c = rmsnorm(q_c, q_norm).reshape(B, Lc, N_KV_HEADS, Q_PER_KV, HEAD_DIM)
    o_c = gqa_attend(q_c, k_c, v_c).reshape(B, Lc, Q_DIM)
    y_c = jnp.concatenate([pool_mix(u_c, pool_w, pool_scale), o_c], axis=-1) @ w_out
    return y_l, y_c


def short_conv(x, w):
    pad = CONV_K // 2
    return lax.conv_general_dilated(x, w[:, None, :], window_strides=(1,), padding=[(pad, pad)],
                                    dimension_numbers=('NWC', 'WIO', 'NWC'),
                                    feature_group_count=x.shape[-1])


def to_chunks(a):
    a = a.reshape((a.shape[0], -1, DN_CHUNK) + a.shape[2:])
    return a.transpose((1, 0, 3, 2) + tuple(range(4, a.ndim)))


def gated_delta_chunked(q, k, v, g, beta, state0):
    with_out = q is not None
    B, L = k.shape[:2]
    causal = jnp.tril(jnp.ones((DN_CHUNK, DN_CHUNK), bool))
    strict = jnp.tril(jnp.ones((DN_CHUNK, DN_CHUNK), bool), -1)
    eye = jnp.eye(DN_CHUNK, dtype=jnp.float32)
    xs = (to_chunks(k), to_chunks(v), to_chunks(g), to_chunks(beta)) + ((to_chunks(q),) if with_out else ())

    def step(S, xc):
        kc = jnp.repeat(xc[0], V_PER_K, axis=1)
        vc, gch, bc = xc[1], xc[2], xc[3]
        gcum = jnp.cumsum(gch, axis=-1)
        decay = jnp.exp(jnp.where(causal, gcum[..., :, None] - gcum[..., None, :], -jnp.inf))
        kb = kc * bc[..., None]
        lower = jnp.where(strict, jnp.einsum('bhid,bhjd->bhij', kb, kc) * decay, 0.0)
        rhs = jnp.concatenate([vc * bc[..., None], kb * jnp.exp(gcum)[..., None]], axis=-1)
        sol = lax.linalg.triangular_solve(eye + lower, rhs, left_side=True, lower=True)
        u, w = sol[..., :DN_DV], sol[..., DN_DV:]
        v_new = u - jnp.einsum('bhcd,bhde->bhce', w, S)
        g_last = gcum[..., -1:]
        S_next = S * jnp.exp(g_last)[..., None] + jnp.einsum(
            'bhcd,bhce->bhde', kc * jnp.exp(g_last - gcum)[..., None], v_new)
        if not with_out:
            return S_next, None
        qc = jnp.repeat(xc[4], V_PER_K, axis=1)
        attn = jnp.einsum('bhid,bhjd->bhij', qc, kc) * decay
        o = (jnp.einsum('bhcd,bhde->bhce', qc * jnp.exp(gcum)[..., None], S)
             + jnp.einsum('bhij,bhje->bhie', attn, v_new))
        return S_next, o

    S_fin, o = lax.scan(step, state0, xs)
    if not with_out:
        return S_fin, None
    return S_fin, o.transpose(1, 0, 3, 2, 4).reshape(B, L, DN_V_HEADS, DN_DV)


def deltanet_prep(h, w_in, conv_w, a_log, dt_bias, with_output):
    B, L = h.shape[:2]
    p = h @ w_in
    qkv = jax.nn.silu(short_conv(p[..., :DN_QKV_DIM], conv_w)).astype(jnp.float32)
    k = l2norm(qkv[..., DN_QK_DIM:2 * DN_QK_DIM].reshape(B, L, DN_K_HEADS, DN_DK))
    v = qkv[..., 2 * DN_QK_DIM:].reshape(B, L, DN_V_HEADS, DN_DV)
    ba = p[..., DN_QKV_DIM + DN_V_DIM:].astype(jnp.float32).reshape(B, L, 2, 2, DN_V_HEADS)
    beta = jax.nn.sigmoid(ba[:, :, 0])
    g = -jnp.exp(a_log.astype(jnp.float32)) * jax.nn.softplus(ba[:, :, 1] + dt_bias.astype(jnp.float32))
    if not with_output:
        return None, k, v, None, g, beta
    q = l2norm(qkv[..., :DN_QK_DIM].reshape(B, L, DN_K_HEADS, DN_DK)) * (DN_DK ** -0.5)
    z = p[..., DN_QKV_DIM:DN_QKV_DIM + DN_V_DIM].reshape(B, L, DN_V_HEADS, DN_DV)
    return q, k, v, z, g, beta


def flip_if(a, reverse):
    return jnp.flip(a, axis=1) if reverse else a


def gated_out(o, z, gnorm, w_out, dtype):
    B, L = o.shape[:2]
    y = rmsnorm(o, gnorm) * jax.nn.silu(z.astype(jnp.float32))
    return y.reshape(B, L, DN_V_DIM).astype(dtype) @ w_out


def deltanet_mixer(h_lat, h_ctx, w_in, conv_w, a_log, dt_bias, gnorm, w_out, need_ctx):
    B = h_lat.shape[0]
    ql, kl, vl, zl, gl, bl = deltanet_prep(h_lat, w_in, conv_w, a_log, dt_bias, True)
    qc, kc, vc, zc, gc, bc = deltanet_prep(h_ctx, w_in, conv_w, a_log, dt_bias, need_ctx)
    outs_l, outs_c = [], []
    for d in range(2):
        s0 = jnp.zeros((B, DN_V_HEADS, DN_DK, DN_DV), jnp.float32)
        s_ctx, o_c = gated_delta_chunked(flip_if(qc, d) if need_ctx else None, flip_if(kc, d), flip_if(vc, d),
                                         flip_if(gc[:, :, d], d), flip_if(bc[:, :, d], d), s0)
        _, o_l = gated_delta_chunked(flip_if(ql, d), flip_if(kl, d), flip_if(vl, d),
                                     flip_if(gl[:, :, d], d), flip_if(bl[:, :, d], d), s_ctx)
        outs_l.append(flip_if(o_l, d))
        if need_ctx:
            outs_c.append(flip_if(o_c, d))
    y_l = gated_out(outs_l[0] + outs_l[1], zl, gnorm, w_out, h_lat.dtype)
    if not need_ctx:
        return y_l, None
    y_c = gated_out(outs_c[0] + outs_c[1], zc, gnorm, w_out, h_ctx.dtype)
    return y_l, y_c


def channel_sublayer(x, mods, gains, w_up, w_down):
    h = modulate(rmsnorm(x, gains[2]), mods[3], mods[4])
    y = jnp.square(jax.nn.relu(h @ w_up)) @ w_down
    return x + mods[5] * rmsnorm(y, gains[3])


def setup_inputs(seed: int = 0) -> dict:
    key = jax.random.key(seed)
    ks = jax.random.split(key, 24)
    f32 = jnp.float32
    n_even = (DEPTH + 1) // 2
    n_odd = DEPTH // 2

    def nrm(k, shape, scale):
        return jax.random.normal(k, shape, f32) * scale

    dt = jnp.exp(jax.random.uniform(ks[16], (n_odd, 2, DN_V_HEADS), f32, math.log(1e-3), math.log(1e-1)))
    return {
        'x': nrm(ks[0], (BATCH, SEQ, D_MODEL), 1.0),
        'c': nrm(ks[1], (BATCH, D_MODEL), 1.0),
        'ctx': nrm(ks[2], (BATCH, CTX_LEN, D_MODEL), 1.0),
        'c_ctx': nrm(ks[3], (D_MODEL,), 1.0),
        'w_mod': nrm(ks[4], (DEPTH, D_MODEL, N_MOD * D_MODEL), 0.5 * D_MODEL ** -0.5),
        'b_mod': nrm(ks[5], (DEPTH, N_MOD * D_MODEL), 0.02),
        'norm_gains': 1.0 + nrm(ks[6], (DEPTH, 4, D_MODEL), 0.05),
        'w_in_ab': nrm(ks[7], (n_even, D_MODEL, AB_IN), D_MODEL ** -0.5),
        'pool_w': nrm(ks[8], (n_even, N_POOL, POOL_GROUP, POOL_GROUP), POOL_GROUP ** -0.5),
        'pool_scale': 1.0 + nrm(ks[9], (n_even, POOL_DIM), 0.1),
        'q_norm': 1.0 + nrm(ks[10], (n_even, HEAD_DIM), 0.05),
        'k_norm': 1.0 + nrm(ks[11], (n_even, HEAD_DIM), 0.05),
        'w_out_ab': nrm(ks[12], (n_even, AB_OUT, D_MODEL), AB_OUT ** -0.5),
        'w_in_c': nrm(ks[13], (n_odd, D_MODEL, C_IN), D_MODEL ** -0.5),
        'conv_c': nrm(ks[14], (n_odd, CONV_K, DN_QKV_DIM), CONV_K ** -0.5),
        'a_log_c': jnp.log(jax.random.uniform(ks[15], (n_odd, 2, DN_V_HEADS), f32, 1.0, 16.0)),
        'dt_bias_c': dt + jnp.log(-jnp.expm1(-dt)),
        'gnorm_c': 1.0 + nrm(ks[17], (n_odd, DN_DV), 0.05),
        'w_out_c': nrm(ks[18], (n_odd, DN_V_DIM, D_MODEL), DN_V_DIM ** -0.5),
        'w_up': nrm(ks[19], (DEPTH, D_MODEL, D_FF), D_MODEL ** -0.5),
        'w_down': nrm(ks[20], (DEPTH, D_FF, D_MODEL), D_FF ** -0.5),
    }


def reference(x, c, ctx, c_ctx, w_mod, b_mod, norm_gains, w_in_ab, pool_w, pool_scale, q_norm, k_norm,
              w_out_ab, w_in_c, conv_c, a_log_c, dt_bias_c, gnorm_c, w_out_c, w_up, w_down):
    cos, sin = axial_rope_tables(x.shape[1])
    silu_c = jax.nn.silu(c)
    silu_cc = jax.nn.silu(c_ctx)
    x_lat, x_ctx = x, ctx
    for layer in range(DEPTH):
        need_ctx = layer < DEPTH - 1
        gains = norm_gains[layer]
        mods_l = jnp.split((silu_c @ w_mod[layer] + b_mod[layer])[:, None, :], N_MOD, axis=-1)
        mods_c = jnp.split((silu_cc @ w_mod[layer] + b_mod[layer])[None, None, :], N_MOD, axis=-1)
        h_l = modulate(rmsnorm(x_lat, gains[0]), mods_l[0], mods_l[1])
        h_c = modulate(rmsnorm(x_ctx, gains[0]), mods_c[0], mods_c[1])
        j = layer // 2
        if layer % 2 == 0:
            y_l, y_c = pool_attn_mixer(h_l, h_c, w_in_ab[j], pool_w[j], pool_scale[j], q_norm[j], k_norm[j],
                                       w_out_ab[j], cos, sin, need_ctx)
        else:
            y_l, y_c = deltanet_mixer(h_l, h_c, w_in_c[j], conv_c[j], a_log_c[j], dt_bias_c[j], gnorm_c[j],
                                      w_out_c[j], need_ctx)
        x_lat = x_lat + mods_l[2] * rmsnorm(y_l, gains[1])
        x_lat = channel_sublayer(x_lat, mods_l, gains, w_up[layer], w_down[layer])
        if need_ctx:
            x_ctx = x_ctx + mods_c[2] * rmsnorm(y_c, gains[1])
            x_ctx = channel_sublayer(x_ctx, mods_c, gains, w_up[layer], w_down[layer])
    return x_lat
```

```python
from contextlib import ExitStack
import numpy as np
import concourse.bass as bass
import concourse.mybir as mybir

F32, BF16 = mybir.dt.float32, mybir.dt.bfloat16
AF = mybir.ActivationFunctionType
ALU = mybir.AluOpType
AX = mybir.AxisListType


class Trk:
    __slots__ = ("rows", "rowlen", "nq", "ng", "qsz", "gsz", "w", "r")

    def __init__(self, shape, nq=4, ng=8):
        self.rows = int(shape[0])
        self.rowlen = int(np.prod(shape[1:])) if len(shape) > 1 else 1
        self.qsz = max(1, -(-self.rows // nq))
        self.nq = -(-self.rows // self.qsz)
        self.gsz = max(1, -(-self.rowlen // ng))
        self.ng = -(-self.rowlen // self.gsz)
        n = self.nq * self.ng
        self.w = [None] * n
        self.r = [dict() for _ in range(n)]

    def cells(self, ap):
        off = int(ap.offset)
        rl = self.rowlen
        r0, f0 = divmod(off, rl)
        rext = 0
        fext = 0
        for step, cnt in ap.ap:
            if cnt <= 1 or step == 0:
                continue
            if step % rl == 0:
                rext += (cnt - 1) * (step // rl)
            else:
                fext += (cnt - 1) * step
        if f0 + fext >= rl:
            r1 = (off + rext * rl + fext) // rl
            g0, g1 = 0, self.ng - 1
        else:
            r1 = r0 + rext
            g0 = f0 // self.gsz
            g1 = (f0 + fext) // self.gsz
        q0 = min(r0 // self.qsz, self.nq - 1)
        q1 = min(r1 // self.qsz, self.nq - 1)
        ng = self.ng
        return [q * ng + g for q in range(q0, q1 + 1) for g in range(g0, g1 + 1)]


class Eng:
    def __init__(self, name, e, sem, same_wait):
        self.name = name
        self.key = name
        self.e = e
        self.sem = sem
        self.n = 0
        self.seen = {}
        self.same_wait = same_wait
        self.slots = []
        self.slot_val = []
        self.slot_i = 0


class K:
    def __init__(self, nc, stack, n_dma_slots=12, same_wait=True):
        self.nc = nc
        self.st = stack
        self.st0 = stack
        self.trk = {}
        self.n_ins = 0
        mk = lambda nm: stack.enter_context(nc.semaphore(nm))
        self.pe = Eng("pe", nc.tensor, mk("s_pe"), False)
        self.dve = Eng("dve", nc.vector, mk("s_dve"), same_wait)
        self.act = Eng("act", nc.scalar, mk("s_act"), same_wait)
        self.pool = Eng("pool", nc.gpsimd, mk("s_pool"), same_wait)
        self.sp = Eng("sp", nc.sync, mk("s_sp"), False)
        for q in (self.sp, self.pool, self.act):
            ns = n_dma_slots if q is not self.act else 4
            for i in range(ns):
                q.slots.append(mk(f"d_{q.name}{i}"))
                q.slot_val.append(0)

    def sbuf(self, name, shape, dtype, ng=8):
        t = self.st.enter_context(self.nc.sbuf_tensor(name, list(shape), dtype))
        self.trk[name] = Trk(shape, 4, ng)
        return t

    def psum(self, name, shape, dtype=F32, ng=4):
        t = self.st.enter_context(self.nc.psum_tensor(name, list(shape), dtype))
        self.trk[name] = Trk(shape, 4, ng)
        return t

    def dram(self, name, shape, dtype, kind="Internal", nq=8, ng=8):
        t = self.nc.dram_tensor(name, list(shape), dtype, kind=kind)
        self.trk[name] = Trk(shape, nq, ng)
        return t.ap()

    def _deps(self, reads, writes):
        deps = {}
        trk = self.trk
        for ap in reads:
            t = trk[ap.tensor.name]
            for c in t.cells(ap):
                w = t.w[c]
                if w is not None and deps.get(w[0], (None, 0))[1] < w[2]:
                    deps[w[0]] = (w[1], w[2])
        for ap in writes:
            t = trk[ap.tensor.name]
            for c in t.cells(ap):
                w = t.w[c]
                if w is not None and deps.get(w[0], (None, 0))[1] < w[2]:
                    deps[w[0]] = (w[1], w[2])
                for tok in t.r[c].values():
                    if deps.get(tok[0], (None, 0))[1] < tok[2]:
                        deps[tok[0]] = (tok[1], tok[2])
        return deps

    def _wait(self, eng, deps):
        for key, (sem, val) in deps.items():
            if key == eng.key and not eng.same_wait:
                continue
            if eng.seen.get(key, 0) < val:
                eng.e.wait_ge(sem, val)
                eng.seen[key] = val

    def _record(self, tok, reads, writes):
        trk = self.trk
        key = tok[0]
        for ap in reads:
            t = trk[ap.tensor.name]
            for c in t.cells(ap):
                t.r[c][key] = tok
        for ap in writes:
            t = trk[ap.tensor.name]
            for c in t.cells(ap):
                t.w[c] = tok
                t.r[c] = {}

    def op(self, eng, fn, reads, writes, inc=True):
        self._wait(eng, self._deps(reads, writes))
        ins = fn(eng.e)
        self.n_ins += 1
        if inc:
            eng.n += 1
            ins.then_inc(eng.sem, 1)
            tok = (eng.key, eng.sem, eng.n)
        else:
            tok = (eng.key, eng.sem, eng.n + 1)
        self._record(tok, reads, writes)
        if inc and eng.n >= 30000:
            eng.prev = (eng.key, eng.sem, eng.n)
            eng.gen = getattr(eng, "gen", 0) + 1
            eng.sem = self.st0.enter_context(self.nc.semaphore(f"s_{eng.name}_{eng.gen}"))
            eng.key = (eng.name, eng.gen)
            eng.n = 0
        return ins

    def dma(self, q, out, in_, **kw):
        deps = self._deps([in_], [out])
        self._wait(q, deps)
        i = q.slot_i % len(q.slots)
        q.slot_i += 1
        sem = q.slots[i]
        key = ("dma", q.name, i)
        prev = q.slot_val[i]
        if prev and q.seen.get(key, 0) < prev:
            q.e.wait_ge(sem, prev)
            q.seen[key] = prev
        q.e.dma_start(out=out, in_=in_, **kw).then_inc(sem, 16)
        self.n_ins += 1
        q.slot_val[i] = prev + 16
        tok = (key, sem, prev + 16)
        self._record(tok, [in_], [out])
        return tok

    def finish(self):
        for q in (self.sp, self.pool, self.act):
            for i, sem in enumerate(q.slots):
                if q.slot_val[i]:
                    self.sp.e.wait_ge(sem, q.slot_val[i])
        for e in (self.pe, self.dve, self.act, self.pool):
            if getattr(e, "prev", None):
                self.sp.e.wait_ge(e.prev[1], e.prev[2])
            if e.n:
                self.sp.e.wait_ge(e.sem, e.n)

    def mm(self, out, lhsT, rhs, start=True, stop=True, inc=None):
        if inc is None:
            inc = True
        return self.op(self.pe, lambda e: e.matmul(out, lhsT=lhsT, rhs=rhs, start=start, stop=stop),
                       [lhsT, rhs], [out], inc=inc)

    def tr(self, out, in_, ident):
        return self.op(self.pe, lambda e: e.transpose(out, in_, ident), [in_, ident], [out])

    def actv(self, out, in_, func, bias=None, scale=1.0, accum_out=None, eng=None):
        rd = [in_]
        kw = {}
        if bias is not None:
            kw["bias"] = bias
            if not isinstance(bias, (int, float)):
                rd.append(bias)
        if not isinstance(scale, (int, float)):
            rd.append(scale)
        wr = [out]
        if accum_out is not None:
            kw["accum_out"] = accum_out
            wr.append(accum_out)
        return self.op(self.act, lambda e: e.activation(out=out, in_=in_, func=func, scale=scale, **kw), rd, wr)

    def tt(self, eng, out, in0, in1, op):
        return self.op(eng, lambda e: e.tensor_tensor(out=out, in0=in0, in1=in1, op=op), [in0, in1], [out])

    def ts(self, eng, out, in0, s1, op0, s2=None, op1=None, accum_out=None):
        rd = [in0] + [s for s in (s1, s2) if s is not None and not isinstance(s, (int, float))]
        wr = [out] + ([accum_out] if accum_out is not None else [])
        kw = {}
        if op1 is not None:
            kw["op1"] = op1
        if accum_out is not None:
            kw["accum_out"] = accum_out
        return self.op(eng, lambda e: e.tensor_scalar(out=out, in0=in0, scalar1=s1, scalar2=s2, op0=op0, **kw), rd, wr)

    def stt(self, eng, out, in0, scalar, in1, op0, op1):
        rd = [in0, in1] + ([scalar] if not isinstance(scalar, (int, float)) else [])
        return self.op(eng, lambda e: e.scalar_tensor_tensor(out=out, in0=in0, scalar=scalar, in1=in1, op0=op0, op1=op1),
                       rd, [out])

    def copy(self, eng, out, in_):
        if eng is self.act:
            return self.op(eng, lambda e: e.copy(out=out, in_=in_), [in_], [out])
        return self.op(eng, lambda e: e.tensor_copy(out=out, in_=in_), [in_], [out])

    def memset(self, eng, ap, val):
        return self.op(eng, lambda e: e.memset(ap, val), [], [ap])


class _Scope:
    def __init__(self, k):
        self.k = k

    def __enter__(self):
        self.old = self.k.st
        self.new = ExitStack()
        self.new.__enter__()
        self.k.st = self.new
        return self

    def __exit__(self, *a):
        self.k.barrier()
        self.k.st = self.old
        return self.new.__exit__(*a)


def _k_scope(self):
    return _Scope(self)


def _k_barrier(self):
    engs = (self.pe, self.dve, self.act, self.pool, self.sp)
    for e in engs:
        for x in engs:
            pv = getattr(x, "prev", None)
            if x is not e and pv and e.seen.get(pv[0], 0) < pv[2]:
                e.e.wait_ge(pv[1], pv[2])
                e.seen[pv[0]] = pv[2]
            if x is not e and x.n and e.seen.get(x.key, 0) < x.n:
                e.e.wait_ge(x.sem, x.n)
                e.seen[x.key] = x.n
        for q in (self.sp, self.pool, self.act):
            for i, sem in enumerate(q.slots):
                key = ("dma", q.name, i)
                if q.slot_val[i] and e.seen.get(key, 0) < q.slot_val[i]:
                    e.e.wait_ge(sem, q.slot_val[i])
                    e.seen[key] = q.slot_val[i]


K.scope = _k_scope
K.barrier = _k_barrier
from concourse.bass_utils import run_bass_kernel_spmd
D = 2048
EPS = 1e-6
TC = 256
NG = 24


def build(T, stages="all"):
    nc = bass.Bass("TRN2", target_bir_lowering=False)
    st = ExitStack()
    st.__enter__()
    k = K(nc, st)
    dve, act, pool, pe, sp = k.dve, k.act, k.pool, k.pe, k.sp
    cnt = [0]

    def nm(p):
        cnt[0] += 1
        return f"{p}{cnt[0]}"

    def ein(name, shape, dt=F32):
        t = nc.dram_tensor(name, list(shape), dt, kind="ExternalInput").ap()
        k.trk[name] = Trk(shape, 8, 8)
        return t

    x = ein("x", [T, D])
    ctx = ein("ctx", [TC, D])
    cc = ein("cc", [128, 32])
    w_mod = ein("w_mod", [2, D, 12288])
    b_mod = ein("b_mod", [2, 12288])
    gains = ein("gains", [2, 4, D])
    w_in_ab = ein("w_in_ab", [D, 2560])
    pool_w = ein("pool_w", [1024, 256])
    pool_scale = ein("pool_scale", [128, 8])
    qk_norm = ein("qk_norm", [2, 128])
    w_out_ab = ein("w_out_ab", [2048, D])
    w_in_c = ein("w_in_c", [D, 12416])
    conv_c = ein("conv_c", [128, 64 * 5])
    adt = ein("adt", [2, 64])
    gnorm = ein("gnorm", [1, 128])
    w_out_c = ein("w_out_c", [4096, D])
    w_up = ein("w_up", [2, D, 8192])
    w_down = ein("w_down", [2, 8192, D])
    cos_l = ein("cos_l", [T, 64])
    sin_l = ein("sin_l", [T, 64])
    cos_c = ein("cos_c", [TC, 64])
    sin_c = ein("sin_c", [TC, 64])
    invc_l = ein("invc_l", [4, T])
    invc_c = ein("invc_c", [4, TC])
    masks = ein("masks", [128, 19 * 128])
    out = nc.dram_tensor("out", [T, D], F32, kind="ExternalOutput").ap()
    k.trk["out"] = Trk([T, D], 8, 8)

    mods_d = k.dram("mods_d", [2, 2, 12288], F32)
    wb_in_ab = k.dram("wb_in_ab", [D, 2560], BF16)
    wb_pool = k.dram("wb_pool", [1024, 256], BF16)
    wb_out_ab = k.dram("wb_out_ab", [2048, D], BF16)
    wb_in_c = k.dram("wb_in_c", [D, 12416], BF16)
    wb_out_c = k.dram("wb_out_c", [4096, D], BF16)
    wb_up = [k.dram(f"wb_up{l}", [D, 8192], BF16) for l in range(2)]
    wb_down = [k.dram(f"wb_down{l}", [8192, D], BF16) for l in range(2)]
    qT_l = k.dram("qT_l", [8, 128, T], BF16)
    qT_c = k.dram("qT_c", [8, 128, TC], BF16)
    uT_l = k.dram("uT_l", [1024, T + 16], F32)
    uT_c = k.dram("uT_c", [1024, TC + 16], F32)
    xm_l = k.dram("xm_l", [T, D], F32)
    xm_c = k.dram("xm_c", [TC, D], F32)
    x1_l = k.dram("x1_l", [T, D], F32)
    x1_c = k.dram("x1_c", [TC, D], F32)
    x2_l = k.dram("x2_l", [T, D], F32)

    pb = [k.psum(f"pb{i}", [128, 512], F32) for i in range(8)]

    ident = k.sbuf("ident", [128, 128], F32)
    identb = k.sbuf("identb", [128, 128], BF16)
    onesb = k.sbuf("onesb", [128, 128], BF16)
    k.dma(sp, ident[:], masks[:, 0:128])
    k.copy(dve, identb[:], ident[:])
    k.memset(dve, onesb[:], 1.0)

    def cast_w(dst, src, rows, cols):
        i = 0
        for r in range(0, rows, 128):
            for c0 in range(0, cols, 2048):
                cw_ = min(2048, cols - c0)
                f32t = cst32[i % 3]
                b16t = cst16[i % 3]
                k.dma(sp, f32t[:, 0:cw_], src[r:r + 128, c0:c0 + cw_])
                eng = (act, dve, pool)[i % 3]
                k.copy(eng, b16t[:, 0:cw_], f32t[:, 0:cw_])
                k.dma(sp, dst[r:r + 128, c0:c0 + cw_], b16t[:, 0:cw_])
                i += 1

    with k.scope():
        cst32 = [k.sbuf(f"cst32_{i}", [128, 2048], F32) for i in range(3)]
        cst16 = [k.sbuf(f"cst16_{i}", [128, 2048], BF16) for i in range(3)]
        if stages != "l1":
            cast_w(wb_in_ab, w_in_ab, D, 2560)
            cast_w(wb_pool, pool_w, 1024, 256)
            cast_w(wb_out_ab, w_out_ab, 2048, D)
            cast_w(wb_up[0], w_up[0], D, 8192)
            cast_w(wb_down[0], w_down[0], 8192, D)
        if stages != "l0":
            cast_w(wb_in_c, w_in_c, D, 12416)
            cast_w(wb_out_c, w_out_c, 4096, D)
            cast_w(wb_up[1], w_up[1], D, 8192)
            cast_w(wb_down[1], w_down[1], 8192, D)

    with k.scope():
        ccs = k.sbuf("ccs", [128, 32], F32)
        sc = k.sbuf("sc", [128, 32], F32)
        bm = k.sbuf("bm", [2, 12288], F32, ng=24)
        wm = [k.sbuf(f"wm{i}", [128, 16, 512], F32) for i in range(2)]
        k.dma(sp, ccs[:], cc[:])
        k.actv(sc[:], ccs[:], AF.Silu)
        for l in range(2):
            k.dma(sp, bm[0:1, :], b_mod[l:l + 1, :])
            k.dma(sp, bm[1:2, :], b_mod[l:l + 1, :])
            wv = w_mod[l].rearrange("(kk p) n -> p kk n", p=128)
            for n in range(NG):
                w_t = wm[n % 2]
                k.dma(sp, w_t[:], wv[:, :, n * 512:(n + 1) * 512])
                ps = pb[n % 2]
                for kk in range(16):
                    k.mm(ps[0:2, :], sc[:, kk:32:16], w_t[:, kk, :], start=(kk == 0), stop=(kk == 15))
                k.tt(dve, bm[:, n * 512:(n + 1) * 512], ps[0:2, :], bm[:, n * 512:(n + 1) * 512], ALU.add)
            k.dma(sp, mods_d[l], bm[:])

    def bload(dst, row_ap):
        k.dma(sp, dst, row_ap.partition_broadcast(128))

    def mod_vecs(l, seq, which, g):
        m = mods_d[l, seq]
        if which in ("pre0", "pre1"):
            si, ci, gi = (0, 1, 0) if which == "pre0" else (3, 4, 2)
            A = k.sbuf(nm("A"), [128, D], F32)
            B = k.sbuf(nm("B"), [128, D], F32)
            bload(A[:], m[ci * D:(ci + 1) * D])
            bload(g[:], gains[l, gi, :])
            bload(B[:], m[si * D:(si + 1) * D])
            k.stt(dve, A[:], A[:], 1.0, g[:], ALU.add, ALU.mult)
            return A, B
        gi_m, gi_g = (2, 1) if which == "g0" else (5, 3)
        G = k.sbuf(nm("G"), [128, D], F32)
        bload(G[:], m[gi_m * D:(gi_m + 1) * D])
        bload(g[:], gains[l, gi_g, :])
        k.tt(dve, G[:], G[:], g[:], ALU.mult)
        return G

    def rstd_of(ss, n, eps=EPS):
        k.ts(dve, ss, ss, 1.0 / n, ALU.mult, eps, ALU.add)
        k.actv(ss, ss, AF.Sqrt)
        k.op(dve, lambda e: e.reciprocal(out=ss, in_=ss), [ss], [ss])

    def norm_mod(xt, A, B, h, junk, ss):
        k.actv(junk[:], xt[:], AF.Square, accum_out=ss[:, 0:1])
        rstd_of(ss[:, 0:1], D)
        k.stt(dve, junk[:], xt[:], ss[:, 0:1], A[:], ALU.mult, ALU.mult)
        k.tt(pool, h[:], junk[:], B[:], ALU.add)

    def transpose_to(hT, h, ncol, tsl, pbs):
        for c4 in range(0, ncol, 4):
            ps = pbs[(c4 // 4) % len(pbs)]
            n4 = min(4, ncol - c4)
            for j in range(n4):
                k.tr(ps[:, j * 128:(j + 1) * 128], h[:, (c4 + j) * 128:(c4 + j + 1) * 128], ident[:])
            eng = act if (c4 // 4) % 2 == 0 else dve
            k.copy(eng, hT[:, c4:c4 + n4, tsl], ps[:, 0:n4 * 128].rearrange("p (c t) -> p c t", t=128))

    def residual_out(zps_list, xt, G, dst_rows, z, ss, sqj):
        for n in range(4):
            k.copy(act if n % 2 == 0 else dve, z[:, n * 512:(n + 1) * 512], zps_list[n][:])
        k.actv(sqj[:], z[:], AF.Square, accum_out=ss[:, 0:1])
        rstd_of(ss[:, 0:1], D)
        k.stt(dve, z[:], z[:], ss[:, 0:1], G[:], ALU.mult, ALU.mult)
        k.tt(pool, z[:], z[:], xt[:], ALU.add)
        k.dma(sp, dst_rows, z[:])

    NKT = (TC + T) // 128
    kT_s = k.dram("kT_s", [2, 128, TC + T], BF16)
    v_s = k.dram("v_s", [TC + T, 256], BF16)
    if stages != "l1":
        with k.scope():
            win = k.sbuf("win", [128, 16, 2560], BF16, ng=16)
            k.dma(sp, win[:], wb_in_ab.rearrange("(kk p) n -> p kk n", p=128))
            qg = k.sbuf("qg", [128, 128], F32)
            kg = k.sbuf("kg", [128, 128], F32)
            bload(qg[:], qk_norm[0, :])
            bload(kg[:], qk_norm[1, :])
            zt = k.sbuf("zt", [128, 8, 8], F32)
            k.memset(dve, zt[:], 0.0)
            for (uT, Tn) in ((uT_l, T), (uT_c, TC)):
                for off in (0, Tn + 8):
                    k.dma(sp, uT[:, off:off + 8].rearrange("(c p) t -> p c t", p=128), zt[:])
            xt_b = [k.sbuf(f"s0x{i}", [128, D], F32) for i in range(2)]
            junk = k.sbuf("s0junk", [128, D], F32)
            h = k.sbuf("s0h", [128, D], F32)
            hT = k.sbuf("s0hT", [128, 16, 128], BF16)
            ss = k.sbuf("s0ss", [128, 16], F32)
            qf = k.sbuf("s0qf", [128, 10, 128], F32)
            qr = k.sbuf("s0qr", [128, 10, 128], F32)
            t1 = k.sbuf("s0t1", [128, 10, 64], F32)
            t2 = k.sbuf("s0t2", [128, 10, 64], F32)
            cs = k.sbuf("s0cs", [128, 2, 64], F32)
            qTt = k.sbuf("s0qTt", [128, 8, 128], BF16)
            ut = k.sbuf("s0ut", [128, 8, 128], F32)
            kTt = k.sbuf("s0kTt", [128, 2, 128], BF16)
            vt = k.sbuf("s0vt", [128, 256], BF16)
            for (xin, Tn, seq, cos_t, sin_t, qT_s, uT_s, ko) in (
                    (ctx, TC, 1, cos_c, sin_c, qT_c, uT_c, 0), (x, T, 0, cos_l, sin_l, qT_l, uT_l, TC)):
                with k.scope():
                    A, B = mod_vecs(0, seq, "pre0", junk)
                    for ti in range(Tn // 128):
                        t0 = ti * 128
                        xt = xt_b[ti % 2]
                        k.dma(sp, xt[:], xin[t0:t0 + 128, :])
                        k.dma(sp, cs[:, 0, :], cos_t[t0:t0 + 128, :])
                        k.dma(sp, cs[:, 1, :], sin_t[t0:t0 + 128, :])
                        norm_mod(xt, A, B, h, junk, ss)
                        transpose_to(hT, h, 16, slice(0, 128), [pb[6], pb[7]])
                        for n in range(3):
                            for kk in range(16):
                                k.mm(pb[n][:], hT[:, kk, :], win[:, kk, 1024 + n * 512:1024 + (n + 1) * 512],
                                     start=(kk == 0), stop=(kk == 15))
                        k.copy(act, vt[:], pb[2][:, 256:512])
                        k.dma(sp, v_s[ko + t0:ko + t0 + 128, :], vt[:])
                        for n in range(2):
                            k.copy(act, qf[:, n * 4:(n + 1) * 4, :].rearrange("p h d -> p (h d)"), pb[n][:])
                        k.copy(act, qf[:, 8:10, :].rearrange("p h d -> p (h d)"), pb[2][:, 0:256])
                        k.tt(dve, qr[:], qf[:], qf[:], ALU.mult)
                        k.op(dve, lambda e: e.tensor_reduce(out=ss[:, 1:11], in_=qr[:], axis=AX.X, op=ALU.add),
                             [qr[:]], [ss[:, 1:11]])
                        rstd_of(ss[:, 1:11], 128)
                        k.tt(dve, qf[:], qf[:], ss[:, 1:11].unsqueeze(2).to_broadcast([128, 10, 128]), ALU.mult)
                        k.tt(dve, qf[:, 0:8, :], qf[:, 0:8, :], qg[:].unsqueeze(1).to_broadcast([128, 8, 128]), ALU.mult)
                        k.tt(dve, qf[:, 8:10, :], qf[:, 8:10, :], kg[:].unsqueeze(1).to_broadcast([128, 2, 128]), ALU.mult)
                        cb = cs[:, 0, :].unsqueeze(1).to_broadcast([128, 10, 64])
                        sb = cs[:, 1, :].unsqueeze(1).to_broadcast([128, 10, 64])
                        x1 = qf[:, :, 0:64]
                        x2 = qf[:, :, 64:128]
                        k.tt(dve, t1[:], x1, cb, ALU.mult)
                        k.tt(pool, t2[:], x2, sb, ALU.mult)
                        k.tt(dve, qr[:, :, 0:64], t1[:], t2[:], ALU.subtract)
                        k.tt(dve, t1[:], x1, sb, ALU.mult)
                        k.tt(pool, t2[:], x2, cb, ALU.mult)
                        k.tt(dve, qr[:, :, 64:128], t1[:], t2[:], ALU.add)
                        qrf = qr[:].rearrange("p h d -> p (h d)")
                        transpose_to(qTt, qrf, 8, slice(0, 128), [pb[3], pb[4]])
                        k.dma(sp, qT_s[:, :, t0:t0 + 128].rearrange("h d t -> d h t"), qTt[:])
                        transpose_to(kTt, qrf[:, 1024:1280], 2, slice(0, 128), [pb[5]])
                        k.dma(sp, kT_s[:, :, ko + t0:ko + t0 + 128].rearrange("h d t -> d h t"), kTt[:])
                        for c in range(8):
                            ps = pb[c % 2]
                            for kk in range(16):
                                k.mm(ps[:, 0:128], win[:, kk, c * 128:(c + 1) * 128], hT[:, kk, :],
                                     start=(kk == 0), stop=(kk == 15))
                            k.copy(act if c % 2 == 0 else dve, ut[:, c, :], ps[:, 0:128])
                        k.dma(sp, uT_s[:, 8 + t0:8 + t0 + 128].rearrange("(c p) t -> p c t", p=128), ut[:])

        with k.scope():
            kT_all = k.sbuf("kT_all", [128, 2, TC + T], BF16, ng=16)
            v_all = k.sbuf("v_all", [128, NKT, 256], BF16, ng=16)
            for j in range(2):
                k.dma(sp, kT_all[:, j, :], kT_s[j])
            k.dma(sp, v_all[:], v_s.rearrange("(kt p) c -> p kt c", p=128))
            wo_b = [k.sbuf(f"wo{i}", [128, 16, 512], BF16) for i in range(2)]
            wov = wb_out_ab.rearrange("(c p) n -> p c n", p=128)
            wpl = k.sbuf("wpl", [128, 8, 256], BF16)
            k.dma(sp, wpl[:], wb_pool.rearrange("(c p) n -> p c n", p=128))
            psc = k.sbuf("psc", [128, 8], F32)
            k.dma(sp, psc[:], pool_scale[:])
            catT = k.sbuf("catT", [128, 16, 512], BF16, ng=16)
            qTs = [k.sbuf(f"s1q{i}", [128, 512], BF16) for i in range(2)]
            pT = [k.sbuf(f"s1p{i}", [128, 512], BF16) for i in range(3)]
            rcp = k.sbuf("s1rcp", [128, 512], F32)
            uw = [k.sbuf(f"s1uw{i}", [128, 528], F32) for i in range(2)]
            sA = k.sbuf("s1sA", [128, 528], F32)
            sB = k.sbuf("s1sB", [128, 528], F32)
            icb = k.sbuf("s1icb", [128, 512], F32)
            dT = k.sbuf("s1dT", [128, 2, 512], BF16)
            xt_b = [k.sbuf(f"s1x{i}", [128, D], F32) for i in range(2)]
            junk = k.sbuf("s1junk", [128, D], F32)
            sqj = k.sbuf("s1sqj", [128, D], BF16)
            ss = k.sbuf("s1ss", [128, 4], F32)
            scale = 128.0 ** -0.5
            for (xin, Tn, seq, qT_s, uT_s, invc, kt0, nkt, xm) in (
                    (ctx, TC, 1, qT_c, uT_c, invc_c, 0, TC // 128, xm_c), (x, T, 0, qT_l, uT_l, invc_l, 0, NKT, xm_l)):
                with k.scope():
                    G = mod_vecs(0, seq, "g0", junk)
                    N = min(512, Tn)
                    for qi in range(Tn // N):
                        q0 = qi * N
                        for hh in range(8):
                            j = hh // 4
                            qt = qTs[hh % 2]
                            k.dma(sp, qt[:, 0:N], qT_s[hh, :, q0:q0 + N])
                            po, psm = pb[4 + (hh % 2) * 2], pb[5 + (hh % 2) * 2]
                            for kt in range(kt0, kt0 + nkt):
                                ps = pb[kt % 3]
                                k.mm(ps[:, 0:N], kT_all[:, j, kt * 128:(kt + 1) * 128], qt[:, 0:N])
                                p_t = pT[kt % 3]
                                k.actv(p_t[:, 0:N], ps[:, 0:N], AF.Exp, scale=scale)
                                first, last = kt == kt0, kt == kt0 + nkt - 1
                                k.mm(po[:, 0:N], v_all[:, kt, j * 128:(j + 1) * 128], p_t[:, 0:N], start=first, stop=last)
                                k.mm(psm[:, 0:N], onesb[:], p_t[:, 0:N], start=first, stop=last)
                            k.op(dve, lambda e: e.reciprocal(out=rcp[:, 0:N], in_=psm[:, 0:N]), [psm[:, 0:N]], [rcp[:, 0:N]])
                            k.tt(dve, catT[:, 8 + hh, 0:N], po[:, 0:N], rcp[:, 0:N], ALU.mult)
                        for g, w in enumerate((2, 4, 8, 16)):
                            k.dma(sp, icb[:, 0:N], invc[g, q0:q0 + N].partition_broadcast(128))
                            for c2 in range(2):
                                c = g * 2 + c2
                                u = uw[c % 2]
                                k.dma(sp, u[:, 0:N + 16], uT_s[c * 128:(c + 1) * 128, q0:q0 + N + 16])
                                W = N + 16
                                k.tt(dve, sA[:, 1:W], u[:, 0:W - 1], u[:, 1:W], ALU.add)
                                cur, oth, lo, hi = sA, sB, 1, W
                                for sh in (1, 2, 4)[:g]:
                                    k.tt(dve if sh != 2 else pool, oth[:, lo + sh:hi - sh], cur[:, lo:hi - 2 * sh], cur[:, lo + 2 * sh:hi], ALU.add)
                                    cur, oth, lo, hi = oth, cur, lo + sh, hi - sh
                                k.tt(dve, oth[:, 8:8 + N], cur[:, 8:8 + N], icb[:, 0:N], ALU.mult)
                                k.tt(dve, dT[:, c2, 0:N], oth[:, 8:8 + N], u[:, 8:8 + N], ALU.subtract)
                            for d2 in range(2):
                                ps = pb[d2]
                                for c2 in range(2):
                                    k.mm(ps[:, 0:N], wpl[:, g * 2 + c2, d2 * 128:(d2 + 1) * 128], dT[:, c2, 0:N],
                                         start=(c2 == 0), stop=(c2 == 1))
                                k.ts(dve, catT[:, g * 2 + d2, 0:N], ps[:, 0:N], psc[:, g * 2 + d2:g * 2 + d2 + 1], ALU.mult)
                        for s in range(N // 128):
                            t0 = q0 + s * 128
                            xt = xt_b[s % 2]
                            k.dma(sp, xt[:], xin[t0:t0 + 128, :])
                            zp = [pb[n] for n in range(4)]
                            for n in range(4):
                                wo = wo_b[n % 2]
                                k.dma(sp, wo[:], wov[:, :, n * 512:(n + 1) * 512])
                                for c in range(16):
                                    k.mm(zp[n][:], catT[:, c, s * 128:(s + 1) * 128], wo[:, c, :],
                                         start=(c == 0), stop=(c == 15))
                            residual_out(zp, xt, G, xm[t0:t0 + 128, :], junk, ss, sqj)

    def mlp(l, seq, xin, xout, Tn):
        with k.scope():
            junk = k.sbuf(nm("mj"), [128, D], F32)
            A, B = mod_vecs(l, seq, "pre1", junk)
            G = mod_vecs(l, seq, "g1", junk)
            N = min(512, Tn)
            NS = N // 128
            xt_b = [k.sbuf(nm("mx"), [128, D], F32) for i in range(2)]
            h = k.sbuf(nm("mh"), [128, D], F32)
            sqj = k.sbuf(nm("msq"), [128, D], BF16)
            ss = k.sbuf(nm("mss"), [128, 4], F32)
            hT = k.sbuf(nm("mhT"), [128, 16, 512], BF16, ng=16)
            aT = k.sbuf(nm("maT"), [128, 64, 512], BF16, ng=64)
            rl = k.sbuf(nm("mrl"), [128, 512], F32)
            wu = [k.sbuf(nm("mwu"), [128, 16, 256], BF16) for i in range(2)]
            wd = [k.sbuf(nm("mwd"), [128, D], BF16) for i in range(4)]
            wuv = wb_up[l].rearrange("(kk p) n -> p kk n", p=128)
            for qi in range(Tn // N):
                q0 = qi * N
                for s in range(NS):
                    k.dma(sp, xt_b[s % 2][:], xin[q0 + s * 128:q0 + (s + 1) * 128, :])
                    norm_mod(xt_b[s % 2], A, B, h, junk, ss)
                    transpose_to(hT, h, 16, slice(s * 128, (s + 1) * 128), [pb[6], pb[7]])
                for fb in range(32):
                    w_t = wu[fb % 2]
                    k.dma(sp, w_t[:], wuv[:, :, fb * 256:(fb + 1) * 256])
                    for f4 in range(2):
                        f = fb * 2 + f4
                        ps = pb[f % 4]
                        for kk in range(16):
                            k.mm(ps[:, 0:N], w_t[:, kk, f4 * 128:(f4 + 1) * 128], hT[:, kk, 0:N],
                                 start=(kk == 0), stop=(kk == 15))
                        k.actv(rl[:, 0:N], ps[:, 0:N], AF.Relu)
                        k.tt(dve, aT[:, f, 0:N], rl[:, 0:N], rl[:, 0:N], ALU.mult)
                for half in range(0, NS, 2):
                    ns = min(2, NS - half)
                    for f in range(64):
                        w_t = wd[f % 4]
                        k.dma(sp, w_t[:], wb_down[l][f * 128:(f + 1) * 128, :])
                        for s2 in range(ns):
                            s = half + s2
                            for n in range(4):
                                k.mm(pb[s2 * 4 + n][:], aT[:, f, s * 128:(s + 1) * 128], w_t[:, n * 512:(n + 1) * 512],
                                     start=(f == 0), stop=(f == 63))
                    for s2 in range(ns):
                        s = half + s2
                        t0 = q0 + s * 128
                        k.dma(sp, xt_b[s % 2][:], xin[t0:t0 + 128, :])
                        residual_out([pb[s2 * 4 + n] for n in range(4)], xt_b[s % 2], G, xout[t0:t0 + 128, :], junk, ss, sqj)

    if stages != "l1":
        mlp(0, 1, xm_c, x1_c, TC)
        mlp(0, 0, xm_l, x1_l if stages == "all" else out, T)


    if stages == "l0":
        k.finish()
        st.__exit__(None, None, None)
        return nc, k

    pT_s = {0: [k.dram(f"pT_l{i}", [4096, T + 4], F32, ng=16) for i in range(2)],
            1: [k.dram(f"pT_c{i}", [4096, TC + 4], F32, ng=16) for i in range(2)]}
    zs_s = k.dram("zs_l", [T, 4096], BF16)
    gb_s = {0: k.dram("gb_l", [T, 128], F32), 1: k.dram("gb_c", [TC, 128], F32)}
    qT1 = k.dram("qT1_l", [16, 128, T], BF16)
    kT1 = {0: k.dram("kT1_l", [16, 128, T], BF16), 1: k.dram("kT1_c", [16, 128, TC], BF16)}
    k1 = {0: k.dram("k1_l", [T, 16 * 128], BF16), 1: k.dram("k1_c", [TC, 16 * 128], BF16)}
    v1 = {0: k.dram("v1_l", [T, 32 * 128], BF16), 1: k.dram("v1_c", [TC, 32 * 128], BF16)}
    o_d = [k.dram(f"o_d{d}", [T, 4096], F32) for d in range(2)]
    seqs = ((1, x1_c, TC), (0, x1_l, T))

    with k.scope():
        junk = k.sbuf("d0junk", [128, D], F32)
        h = k.sbuf("d0h", [128, D], F32)
        xt_b = [k.sbuf(f"d0x{i}", [128, D], F32) for i in range(2)]
        ss = k.sbuf("d0ss", [128, 4], F32)
        hT = k.sbuf("d0hT", [128, 16, 2048], BF16, ng=16)
        wblk = [k.sbuf(f"d0w{i}", [128, 16, 512], BF16) for i in range(2)]
        stg = [k.sbuf(f"d0stg{i}", [128, 512], F32) for i in range(3)]
        zst = [k.sbuf(f"d0z{i}", [128, 512], BF16) for i in range(2)]
        gbt = k.sbuf("d0gb", [128, 128], F32)
        tmp64 = k.sbuf("d0t64", [128, 64], F32)
        adtb = k.sbuf("d0adt", [128, 2, 64], F32)
        zpad = k.sbuf("d0zp", [128, 64, 2], F32)
        bload(adtb[:, 0, :], adt[0, :])
        bload(adtb[:, 1, :], adt[1, :])
        k.actv(adtb[:, 0, :], adtb[:, 0, :], AF.Exp)
        k.memset(dve, zpad[:], 0.0)
        wcv = wb_in_c.rearrange("(kk p) n -> p kk n", p=128)
        for (seq, xin, Tn) in seqs:
            pT = pT_s[seq]
            for off in (0, Tn + 2):
                for i in range(2):
                    k.dma(sp, pT[i][:, off:off + 2].rearrange("(c p) t -> p c t", p=128), zpad[:, 0:32, :])
            with k.scope():
                A, B = mod_vecs(1, seq, "pre0", junk)
                for s0 in range(0, Tn, 2048):
                    ST = min(2048, Tn - s0)
                    for ti in range(ST // 128):
                        xt = xt_b[ti % 2]
                        k.dma(sp, xt[:], xin[s0 + ti * 128:s0 + (ti + 1) * 128, :])
                        norm_mod(xt, A, B, h, junk, ss)
                        transpose_to(hT, h, 16, slice(ti * 128, (ti + 1) * 128), [pb[6], pb[7]])
                    NB = 25 if seq == 0 else 25
                    for blk in range(25):
                        if seq == 1 and 16 <= blk < 24:
                            continue
                        ncol = 512 if blk < 24 else 128
                        w_t = wblk[blk % 2]
                        k.dma(sp, w_t[:, :, 0:ncol], wcv[:, :, blk * 512:blk * 512 + ncol])
                        if blk < 16:
                            for c4 in range(4):
                                c = blk * 4 + c4
                                for sub in range(0, ST, 512):
                                    N = min(512, ST - sub)
                                    ps = pb[(c4 + sub // 512) % 4]
                                    for kk in range(16):
                                        k.mm(ps[:, 0:N], w_t[:, kk, c4 * 128:(c4 + 1) * 128], hT[:, kk, sub:sub + N],
                                             start=(kk == 0), stop=(kk == 15))
                                    sg = stg[(c4 + sub // 512) % 3]
                                    k.copy(act if c4 % 2 == 0 else dve, sg[:, 0:N], ps[:, 0:N])
                                    k.dma(sp, pT[c // 32][(c % 32) * 128:(c % 32 + 1) * 128, 2 + s0 + sub:2 + s0 + sub + N], sg[:, 0:N])
                        elif blk < 24:
                            zb = blk - 16
                            for ti in range(ST // 128):
                                ps = pb[ti % 4]
                                for kk in range(16):
                                    k.mm(ps[:], hT[:, kk, ti * 128:(ti + 1) * 128], w_t[:, kk, :],
                                         start=(kk == 0), stop=(kk == 15))
                                zt_ = zst[ti % 2]
                                k.actv(zt_[:], ps[:], AF.Silu)
                                k.dma(sp, zs_s[s0 + ti * 128:s0 + (ti + 1) * 128, zb * 512:(zb + 1) * 512], zt_[:])
                        else:
                            for ti in range(ST // 128):
                                ps = pb[ti % 4]
                                for kk in range(16):
                                    k.mm(ps[:, 0:128], hT[:, kk, ti * 128:(ti + 1) * 128], w_t[:, kk, 0:128],
                                         start=(kk == 0), stop=(kk == 15))
                                k.actv(gbt[:, 0:64], ps[:, 0:64], AF.Sigmoid)
                                k.tt(dve, tmp64[:], ps[:, 64:128], adtb[:, 1, :], ALU.add)
                                k.actv(tmp64[:], tmp64[:], AF.Exp)
                                k.actv(tmp64[:], tmp64[:], AF.Ln, bias=1.0)
                                k.stt(dve, gbt[:, 64:128], tmp64[:], -1.0, adtb[:, 0, :], ALU.mult, ALU.mult)
                                k.dma(sp, gb_s[seq][s0 + ti * 128:s0 + (ti + 1) * 128, :], gbt[:])

    with k.scope():
        cw = k.sbuf("d1cw", [128, 320], F32)
        k.dma(sp, cw[:], conv_c[:])
        pw = [k.sbuf(f"d1pw{i}", [128, 516], F32) for i in range(2)]
        acc = k.sbuf("d1acc", [128, 512], F32)
        sv = k.sbuf("d1sv", [128, 512], F32)
        sqb = k.sbuf("d1sqb", [128, 512], BF16)
        rn = k.sbuf("d1rn", [128, 512], F32)
        xn = k.sbuf("d1xn", [128, 512], F32)
        xnb = [k.sbuf(f"d1xnb{i}", [128, 512], BF16) for i in range(2)]
        tok = [k.sbuf(f"d1tok{i}", [128, 4, 128], BF16) for i in range(2)]
        for (seq, xin, Tn) in seqs:
            pT = pT_s[seq]
            N = min(512, Tn)
            for q0 in range(0, Tn, N):
                for c in range(64):
                    if seq == 1 and c < 16:
                        continue
                    p_t = pw[c % 2]
                    k.dma(sp, p_t[:, 0:N + 4], pT[c // 32][(c % 32) * 128:(c % 32 + 1) * 128, q0:q0 + N + 4])
                    k.ts(dve, acc[:, 0:N], p_t[:, 0:N], cw[:, c * 5:c * 5 + 1], ALU.mult)
                    for j in range(1, 5):
                        k.stt(dve, acc[:, 0:N], p_t[:, j:j + N], cw[:, c * 5 + j:c * 5 + j + 1], acc[:, 0:N], ALU.mult, ALU.add)
                    k.actv(sv[:, 0:N], acc[:, 0:N], AF.Silu)
                    src = sv
                    if c < 32:
                        k.tt(pool, sqb[:, 0:N], sv[:, 0:N], sv[:, 0:N], ALU.mult)
                        ps = pb[c % 2]
                        k.mm(ps[:, 0:N], onesb[:], sqb[:, 0:N])
                        k.ts(dve, rn[:, 0:N], ps[:, 0:N], EPS, ALU.add)
                        k.actv(rn[:, 0:N], rn[:, 0:N], AF.Sqrt)
                        k.op(dve, lambda e: e.reciprocal(out=rn[:, 0:N], in_=rn[:, 0:N]), [rn[:, 0:N]], [rn[:, 0:N]])
                        if c < 16:
                            k.stt(dve, xn[:, 0:N], sv[:, 0:N], 128.0 ** -0.5, rn[:, 0:N], ALU.mult, ALU.mult)
                        else:
                            k.tt(dve, xn[:, 0:N], sv[:, 0:N], rn[:, 0:N], ALU.mult)
                        xb = xnb[c % 2]
                        k.copy(act, xb[:, 0:N], xn[:, 0:N])
                        if c < 16:
                            k.dma(sp, qT1[c, :, q0:q0 + N], xb[:, 0:N])
                            continue
                        k.dma(sp, kT1[seq][c - 16, :, q0:q0 + N], xb[:, 0:N])
                        src = xn
                    tk = tok[c % 2]
                    ps = pb[2 + c % 2]
                    for j in range(N // 128):
                        k.tr(ps[:, j * 128:(j + 1) * 128], src[:, j * 128:(j + 1) * 128], ident[:])
                    k.copy(act if c % 2 == 0 else dve, tk[:, 0:N // 128, :], ps[:, 0:N].rearrange("p (a b) -> p a b", b=128))
                    if c < 32:
                        dst = k1[seq][q0:q0 + N, (c - 16) * 128:(c - 15) * 128]
                    else:
                        dst = v1[seq][q0:q0 + N, (c - 32) * 128:(c - 31) * 128]
                    k.dma(sp, dst.rearrange("(a p) d -> p a d", p=128), tk[:, 0:N // 128, :])

    with k.scope():
        mk = k.sbuf("d2mk", [128, 4, 128], F32)
        k.dma(sp, mk[:].rearrange("p a b -> p (a b)"), masks[:, 128:640])
        ones32 = k.sbuf("d2ones", [128, 128], F32)
        k.memset(dve, ones32[:], 1.0)
        S = k.sbuf("d2S", [128, 32, 128], F32, ng=32)
        Sb = k.sbuf("d2Sb", [128, 32, 128], BF16, ng=32)
        gbt = k.sbuf("d2gb", [128, 128], F32)
        kTc = k.sbuf("d2kT", [128, 16, 128], BF16, ng=16)
        qTc = k.sbuf("d2qT", [128, 16, 128], BF16, ng=16)
        ktok = k.sbuf("d2kt", [128, 16, 128], BF16, ng=16)
        vtok = k.sbuf("d2vt", [128, 32, 128], BF16, ng=32)
        kd = k.sbuf("d2kd", [128, 32, 128], BF16, ng=32)
        gc = k.sbuf("d2gc", [128, 32], F32)
        gt = k.sbuf("d2gt", [128, 32], F32)
        eg = k.sbuf("d2eg", [128, 32], F32)
        ed = k.sbuf("d2ed", [128, 32], F32)
        egt = k.sbuf("d2egt", [128, 32], F32)
        Rd = k.sbuf("d2Rd", [128, 8, 128], F32)
        E = k.sbuf("d2E", [128, 8, 128], F32)
        tL = k.sbuf("d2tL", [128, 8, 128], F32)
        N32 = k.sbuf("d2N32", [128, 8, 128], F32)
        A32 = k.sbuf("d2A32", [128, 8, 128], F32)
        P = [k.sbuf(f"d2P{i}", [128, 8, 128], BF16) for i in range(2)]
        Pt = [k.sbuf(f"d2Pt{i}", [128, 8, 128], BF16) for i in range(2)]
        Tt = [k.sbuf(f"d2Tt{i}", [128, 8, 128], BF16) for i in range(2)]
        AtT = k.sbuf("d2AtT", [128, 8, 128], BF16)
        Tn_ = [k.sbuf(f"d2Tn{i}", [128, 8, 128], BF16) for i in range(2)]
        Cb = k.sbuf("d2Cb", [128, 8, 128], BF16)
        Ctb = k.sbuf("d2Ctb", [128, 8, 128], BF16)
        Wb = k.sbuf("d2Wb", [128, 8, 128], BF16)
        W2b = k.sbuf("d2W2b", [128, 8, 128], BF16)
        mq32 = k.sbuf("d2mq32", [128, 14, 128], F32, ng=14)
        k.dma(sp, mq32[:].rearrange("p a b -> p (a b)"), masks[:, 640:640 + 14 * 128])
        mLb = k.sbuf("d2mLb", [128, 7, 128], BF16)
        mUb = k.sbuf("d2mUb", [128, 7, 128], BF16)
        k.copy(dve, mLb[:], mq32[:, 0:7, :])
        k.copy(dve, mUb[:], mq32[:, 7:14, :])
        X32 = k.sbuf("d2X32", [128, 8, 128], F32)
        Xb = k.sbuf("d2Xb", [128, 8, 128], BF16)
        vn = k.sbuf("d2vn", [128, 8, 128], BF16)
        ot = k.sbuf("d2ot", [128, 32, 128], F32, ng=32)

        def v3(ps2):
            return None

        def bank8(i):
            return [pb[i + hh // 4][:, (hh % 4) * 128:(hh % 4 + 1) * 128] for hh in range(8)]

        def b3(i, half):
            return pb[i + half][:].rearrange("p (a b) -> p a b", b=128)

        for d in range(2):
            k.memset(dve, S[:], 0.0)
            k.memset(pool, Sb[:], 0.0)
            m_incl = mk[:, d, :]
            m_strict = mk[:, 2 + d, :]
            m_cumT = mk[:, 1 - d, :]
            order = []
            cc_ = list(range(TC // 128))
            lc_ = list(range(T // 128))
            if d == 1:
                cc_, lc_ = cc_[::-1], lc_[::-1]
            order = [(1, c) for c in cc_] + [(0, c) for c in lc_]
            for (seq, ci) in order:
                t0 = ci * 128
                want_o = seq == 0
                k.dma(sp, gbt[:], gb_s[seq][t0:t0 + 128, :])
                k.dma(sp, kTc[:], kT1[seq][:, :, t0:t0 + 128].rearrange("h d t -> d h t"))
                k.dma(sp, ktok[:].rearrange("p h d -> p (h d)"), k1[seq][t0:t0 + 128, :])
                k.dma(sp, vtok[:].rearrange("p h d -> p (h d)"), v1[seq][t0:t0 + 128, :])
                if want_o:
                    k.dma(sp, qTc[:], qT1[:, :, t0:t0 + 128].rearrange("h d t -> d h t"))
                beta = gbt[:, d * 32:(d + 1) * 32]
                g = gbt[:, 64 + d * 32:64 + (d + 1) * 32]
                k.mm(pb[0][:, 0:32], m_cumT, g)
                k.mm(pb[0][:, 32:64], ones32[:], g)
                k.copy(dve, gc[:], pb[0][:, 0:32])
                k.copy(dve, gt[:], pb[0][:, 32:64])
                k.actv(eg[:], gc[:], AF.Exp)
                k.actv(egt[:], gt[:], AF.Exp)
                k.tt(dve, ed[:], gt[:], gc[:], ALU.subtract)
                k.actv(ed[:], ed[:], AF.Exp)
                k.tt(dve, kd[:].rearrange("p (a b) d -> p a b d", b=2),
                     ktok[:].unsqueeze(2).to_broadcast([128, 16, 2, 128]),
                     ed[:].rearrange("p (a b) -> p a b", b=2).unsqueeze(3).to_broadcast([128, 16, 2, 128]), ALU.mult)
                for hg in range(4):
                    hs = slice(hg * 8, (hg + 1) * 8)
                    gcb = gc[:, hs].unsqueeze(2).to_broadcast([128, 8, 128])
                    k.tt(dve, Rd[:], gcb, ident[:].unsqueeze(1).to_broadcast([128, 8, 128]), ALU.mult)
                    for half in range(2):
                        k.mm(pb[half][:], ones32[:], Rd[:, half * 4:(half + 1) * 4, :].rearrange("p a b -> p (a b)"))
                    for half in range(2):
                        k.tt(dve, E[:, half * 4:(half + 1) * 4, :], gc[:, hg * 8 + half * 4:hg * 8 + half * 4 + 4].unsqueeze(2).to_broadcast([128, 4, 128]),
                             b3(0, half), ALU.subtract)
                    k.ts(pool, E[:], E[:], 0.0, ALU.min)
                    k.actv(E[:], E[:], AF.Exp)
                    k.tt(dve, E[:], E[:], m_incl.unsqueeze(1).to_broadcast([128, 8, 128]), ALU.mult)
                    for a in range(4):
                        hk = hg * 4 + a
                        k.mm(pb[2][:, a * 128:(a + 1) * 128], kTc[:, hk, :], kTc[:, hk, :])
                    if want_o:
                        for a in range(4):
                            hk = hg * 4 + a
                            k.mm(pb[3][:, a * 128:(a + 1) * 128], qTc[:, hk, :], kTc[:, hk, :])
                    E4 = E[:].rearrange("p (a b) j -> p a b j", b=2)
                    KKb = pb[2][:].rearrange("p (a j) -> p a j", j=128).unsqueeze(2).to_broadcast([128, 4, 2, 128])
                    k.tt(dve, tL[:].rearrange("p (a b) j -> p a b j", b=2), E4, KKb, ALU.mult)
                    k.tt(pool, tL[:], tL[:], m_strict.unsqueeze(1).to_broadcast([128, 8, 128]), ALU.mult)
                    k.stt(dve, N32[:], tL[:], -1.0, beta[:, hs].unsqueeze(2).to_broadcast([128, 8, 128]), ALU.mult, ALU.mult)
                    k.copy(act, P[0][:], N32[:])
                    if want_o:
                        QKb = pb[3][:].rearrange("p (a j) -> p a j", j=128).unsqueeze(2).to_broadcast([128, 4, 2, 128])
                        k.tt(dve, A32[:].rearrange("p (a b) j -> p a b j", b=2), E4, QKb, ALU.mult)
                    for hh in range(8):
                        k.tr(bank8(4)[hh], N32[:, hh, :], ident[:])
                    for half in range(2):
                        k.copy(act if half == 0 else dve, Pt[0][:, half * 4:(half + 1) * 4, :], b3(4, half))
                    if want_o:
                        for hh in range(8):
                            k.tr(bank8(6)[hh], A32[:, hh, :], ident[:])
                        for half in range(2):
                            k.copy(act if half == 0 else dve, AtT[:, half * 4:(half + 1) * 4, :], b3(6, half))
                    bc8 = lambda m: m.unsqueeze(1).to_broadcast([128, 8, 128])
                    mA = mLb if d == 0 else mUb
                    mB = mUb if d == 0 else mLb
                    cur = 0
                    k.tt(pool, Cb[:], P[0][:], bc8(mA[:, 0, :]), ALU.mult)
                    k.tt(dve, Tn_[cur][:], Cb[:], bc8(identb[:]), ALU.add)
                    k.tt(pool, Ctb[:], Pt[0][:], bc8(mB[:, 0, :]), ALU.mult)
                    k.tt(dve, Tt[cur][:], Ctb[:], bc8(identb[:]), ALU.add)
                    for lv in range(1, 7):
                        nxt = 1 - cur
                        k.tt(pool, Cb[:], P[0][:], bc8(mA[:, lv, :]), ALU.mult)
                        k.tt(pool, Ctb[:], Pt[0][:], bc8(mB[:, lv, :]), ALU.mult)
                        for hh in range(8):
                            k.mm(bank8(0)[hh], Cb[:, hh, :], Tt[cur][:, hh, :])
                        for hh in range(8):
                            k.mm(bank8(2)[hh], Ctb[:, hh, :], Tn_[cur][:, hh, :])
                        for half in range(2):
                            k.copy(act if half == 0 else dve, Wb[:, half * 4:(half + 1) * 4, :], b3(0, half))
                        for half in range(2):
                            k.copy(dve if half == 0 else act, W2b[:, half * 4:(half + 1) * 4, :], b3(2, half))
                        for hh in range(8):
                            k.mm(bank8(4)[hh], identb[:], Tt[cur][:, hh, :], start=True, stop=False)
                            k.mm(bank8(4)[hh], Tn_[cur][:, hh, :], Wb[:, hh, :], start=False, stop=True)
                        for hh in range(8):
                            k.mm(bank8(6)[hh], identb[:], Tn_[cur][:, hh, :], start=True, stop=False)
                            k.mm(bank8(6)[hh], Tt[cur][:, hh, :], W2b[:, hh, :], start=False, stop=True)
                        for half in range(2):
                            k.copy(act if half == 0 else dve, Tt[nxt][:, half * 4:(half + 1) * 4, :], b3(4, half))
                        for half in range(2):
                            k.copy(dve if half == 0 else act, Tn_[nxt][:, half * 4:(half + 1) * 4, :], b3(6, half))
                        cur = nxt
                    TT = Tt[cur]
                    for hh in range(8):
                        hv = hg * 8 + hh
                        k.mm(bank8(0)[hh], kTc[:, hv // 2, :], Sb[:, hv, :])
                    egb = eg[:, hs].unsqueeze(2).to_broadcast([128, 8, 128])
                    for half in range(2):
                        k.tt(dve, X32[:, half * 4:(half + 1) * 4, :], b3(0, half),
                             eg[:, hg * 8 + half * 4:hg * 8 + half * 4 + 4].unsqueeze(2).to_broadcast([128, 4, 128]), ALU.mult)
                    k.tt(pool, X32[:], vtok[:, hs, :], X32[:], ALU.subtract)
                    k.tt(dve, Xb[:], X32[:], beta[:, hs].unsqueeze(2).to_broadcast([128, 8, 128]), ALU.mult)
                    for hh in range(8):
                        k.mm(bank8(2)[hh], TT[:, hh, :], Xb[:, hh, :])
                    for half in range(2):
                        k.copy(act if half == 0 else dve, vn[:, half * 4:(half + 1) * 4, :], b3(2, half))
                    if want_o:
                        for hh in range(8):
                            hv = hg * 8 + hh
                            k.mm(bank8(4)[hh], qTc[:, hv // 2, :], Sb[:, hv, :])
                        for hh in range(8):
                            k.mm(bank8(6)[hh], AtT[:, hh, :], vn[:, hh, :])
                        for half in range(2):
                            osl = ot[:, hg * 8 + half * 4:hg * 8 + half * 4 + 4, :]
                            k.tt(dve, osl, b3(4, half),
                                 eg[:, hg * 8 + half * 4:hg * 8 + half * 4 + 4].unsqueeze(2).to_broadcast([128, 4, 128]), ALU.mult)
                            k.tt(dve, osl, osl, b3(6, half), ALU.add)
                    for hh in range(8):
                        hv = hg * 8 + hh
                        k.mm(bank8(0)[hh], kd[:, hv, :], vn[:, hh, :])
                    k.tt(pool, S[:, hs, :], S[:, hs, :], egt[:, hs].unsqueeze(2).to_broadcast([128, 8, 128]), ALU.mult)
                    for half in range(2):
                        ssl = S[:, hg * 8 + half * 4:hg * 8 + half * 4 + 4, :]
                        k.tt(dve, ssl, ssl, b3(0, half), ALU.add)
                    k.copy(act, Sb[:, hs, :], S[:, hs, :])
                if want_o:
                    k.dma(sp, o_d[d][t0:t0 + 128, :], ot[:].rearrange("p h d -> p (h d)"))

    with k.scope():
        junk = k.sbuf("d3junk", [128, D], F32)
        G = mod_vecs(1, 0, "g0", junk)
        gnb = k.sbuf("d3gn", [128, 128], F32)
        bload(gnb[:], gnorm[0, :])
        o0 = k.sbuf("d3o0", [128, 32, 128], F32, ng=32)
        o1 = k.sbuf("d3o1", [128, 32, 128], F32, ng=32)
        zt = k.sbuf("d3z", [128, 32, 128], BF16, ng=32)
        yT = k.sbuf("d3yT", [128, 32, 128], BF16, ng=32)
        wo = [k.sbuf(f"d3wo{i}", [128, 32, 512], BF16, ng=32) for i in range(2)]
        xt = k.sbuf("d3x", [128, D], F32)
        sqj = k.sbuf("d3sq", [128, D], BF16)
        ss = k.sbuf("d3ss", [128, 40], F32)
        wov = wb_out_c.rearrange("(c p) n -> p c n", p=128)
        for ti in range(T // 128):
            t0 = ti * 128
            k.dma(sp, o0[:].rearrange("p h d -> p (h d)"), o_d[0][t0:t0 + 128, :])
            k.dma(sp, o1[:].rearrange("p h d -> p (h d)"), o_d[1][t0:t0 + 128, :])
            k.dma(sp, zt[:].rearrange("p h d -> p (h d)"), zs_s[t0:t0 + 128, :])
            k.dma(sp, xt[:], x1_l[t0:t0 + 128, :])
            k.tt(dve, o0[:], o0[:], o1[:], ALU.add)
            k.tt(pool, o1[:], o0[:], o0[:], ALU.mult)
            k.op(dve, lambda e: e.tensor_reduce(out=ss[:, 4:36], in_=o1[:], axis=AX.X, op=ALU.add), [o1[:]], [ss[:, 4:36]])
            rstd_of(ss[:, 4:36], 128)
            k.tt(dve, o0[:], o0[:], ss[:, 4:36].unsqueeze(2).to_broadcast([128, 32, 128]), ALU.mult)
            k.tt(pool, o0[:], o0[:], gnb[:].unsqueeze(1).to_broadcast([128, 32, 128]), ALU.mult)
            k.tt(dve, o0[:], o0[:], zt[:], ALU.mult)
            transpose_to(yT, o0[:].rearrange("p h d -> p (h d)"), 32, slice(0, 128), [pb[4], pb[5], pb[6], pb[7]])
            zp = [pb[n] for n in range(4)]
            for n in range(4):
                w_t = wo[n % 2]
                k.dma(sp, w_t[:], wov[:, :, n * 512:(n + 1) * 512])
                for c in range(32):
                    k.mm(zp[n][:], yT[:, c, :], w_t[:, c, :], start=(c == 0), stop=(c == 31))
            residual_out(zp, xt, G, x2_l[t0:t0 + 128, :], junk, ss, sqj)

    mlp(1, 0, x2_l, out, T)

    k.finish()
    st.__exit__(None, None, None)
    return nc, k


def _tables(T):
    t = np.arange(T)
    row = (t // 64).astype(np.float32)
    col = (t % 64).astype(np.float32)
    inv = (10000.0 ** (-np.arange(32, dtype=np.float32) / 32)).astype(np.float32)
    ang = np.concatenate([row[:, None] * inv, col[:, None] * inv], axis=-1).astype(np.float32)
    return np.cos(ang).astype(np.float32), np.sin(ang).astype(np.float32)


def _invc(L):
    t = np.arange(L)
    rows = []
    for w in (2, 4, 8, 16):
        lo = np.clip(t - w // 2, 0, L)
        hi = np.clip(t + w - w // 2, 0, L)
        rows.append(1.0 / (hi - lo).astype(np.float32))
    return np.stack(rows).astype(np.float32)


def _masks():
    i = np.arange(128)[:, None]
    j = np.arange(128)[None, :]
    ms = [i == j, j <= i, j >= i, j < i, j > i]
    quad = []
    for b in (1, 2, 4, 8, 16, 32, 64):
        quad.append((i // (2 * b) == j // (2 * b)) & (i % (2 * b) >= b) & (j % (2 * b) < b))
    ms = ms + quad + [q.T for q in quad]
    return np.ascontiguousarray(np.concatenate([m.astype(np.float32) for m in ms], axis=1))


def make_in_map(inp, b, T):
    f = lambda a: np.ascontiguousarray(np.asarray(a, dtype=np.float32))
    cos_l, sin_l = _tables(T)
    cc = np.concatenate([f(inp["c"][b]).reshape(16, 128).T, f(inp["c_ctx"]).reshape(16, 128).T], axis=1)
    return {
        "x": f(inp["x"][b][:T]), "ctx": f(inp["ctx"][b]), "cc": f(cc),
        "w_mod": f(inp["w_mod"]), "b_mod": f(inp["b_mod"]), "gains": f(inp["norm_gains"]),
        "w_in_ab": f(inp["w_in_ab"][0]), "pool_w": f(inp["pool_w"][0]).reshape(1024, 256),
        "pool_scale": f(f(inp["pool_scale"][0]).reshape(8, 128).T),
        "qk_norm": f(np.stack([inp["q_norm"][0], inp["k_norm"][0]])),
        "w_out_ab": f(inp["w_out_ab"][0]), "w_in_c": f(inp["w_in_c"][0]),
        "conv_c": f(f(inp["conv_c"][0]).reshape(5, 64, 128).transpose(2, 1, 0).reshape(128, 320)),
        "adt": f(np.stack([f(inp["a_log_c"][0]).reshape(64), f(inp["dt_bias_c"][0]).reshape(64)])),
        "gnorm": f(inp["gnorm_c"][0]).reshape(1, 128), "w_out_c": f(inp["w_out_c"][0]),
        "w_up": f(inp["w_up"]), "w_down": f(inp["w_down"]),
        "cos_l": cos_l, "sin_l": sin_l,
        "cos_c": np.ones((TC, 64), np.float32), "sin_c": np.zeros((TC, 64), np.float32),
        "invc_l": _invc(T), "invc_c": _invc(TC), "masks": _masks(),
    }


_CACHE = {}


def kernel(**inputs):
    T = inputs["x"].shape[1]
    B = inputs["x"].shape[0]
    if T not in _CACHE:
        _CACHE[T] = build(T)
    nc, _ = _CACHE[T]
    n = B
    maps = [make_in_map(inputs, c % B, T) for c in range(n)]
    res = run_bass_kernel_spmd(nc, maps, core_ids=list(range(n)))
    return np.stack([res.results[b]["out"] for b in range(B)]).astype(np.float32)
```

```python
from contextlib import ExitStack
import numpy as np
import concourse.bass as bass
import concourse.mybir as mybir

F32, BF16 = mybir.dt.float32, mybir.dt.bfloat16
AF = mybir.ActivationFunctionType
ALU = mybir.AluOpType
AX = mybir.AxisListType


class Trk:
    __slots__ = ("rows", "rowlen", "nq", "ng", "qsz", "gsz", "w", "r")

    def __init__(self, shape, nq=4, ng=8):
        self.rows = int(shape[0])
        self.rowlen = int(np.prod(shape[1:])) if len(shape) > 1 else 1
        self.qsz = max(1, -(-self.rows // nq))
        self.nq = -(-self.rows // self.qsz)
        self.gsz = max(1, -(-self.rowlen // ng))
        self.ng = -(-self.rowlen // self.gsz)
        n = self.nq * self.ng
        self.w = [None] * n
        self.r = [dict() for _ in range(n)]

    def cells(self, ap):
        off = int(ap.offset)
        rl = self.rowlen
        r0, f0 = divmod(off, rl)
        rext = 0
        fext = 0
        for step, cnt in ap.ap:
            if cnt <= 1 or step == 0:
                continue
            if step % rl == 0:
                rext += (cnt - 1) * (step // rl)
            else:
                fext += (cnt - 1) * step
        if f0 + fext >= rl:
            r1 = (off + rext * rl + fext) // rl
            g0, g1 = 0, self.ng - 1
        else:
            r1 = r0 + rext
            g0 = f0 // self.gsz
            g1 = (f0 + fext) // self.gsz
        q0 = min(r0 // self.qsz, self.nq - 1)
        q1 = min(r1 // self.qsz, self.nq - 1)
        ng = self.ng
        return [q * ng + g for q in range(q0, q1 + 1) for g in range(g0, g1 + 1)]


class Eng:
    def __init__(self, name, e, sem, same_wait):
        self.name = name
        self.key = name
        self.e = e
        self.sem = sem
        self.n = 0
        self.seen = {}
        self.same_wait = same_wait
        self.slots = []
        self.slot_val = []
        self.slot_i = 0


class K:
    def __init__(self, nc, stack, n_dma_slots=12, same_wait=True):
        self.nc = nc
        self.st = stack
        self.st0 = stack
        self.trk = {}
        self.n_ins = 0
        mk = lambda nm: stack.enter_context(nc.semaphore(nm))
        self.pe = Eng("pe", nc.tensor, mk("s_pe"), False)
        self.dve = Eng("dve", nc.vector, mk("s_dve"), same_wait)
        self.act = Eng("act", nc.scalar, mk("s_act"), same_wait)
        self.pool = Eng("pool", nc.gpsimd, mk("s_pool"), same_wait)
        self.sp = Eng("sp", nc.sync, mk("s_sp"), False)
        for q in (self.sp, self.pool, self.act):
            ns = n_dma_slots if q is not self.act else 4
            for i in range(ns):
                q.slots.append(mk(f"d_{q.name}{i}"))
                q.slot_val.append(0)

    def sbuf(self, name, shape, dtype, ng=8):
        t = self.st.enter_context(self.nc.sbuf_tensor(name, list(shape), dtype))
        self.trk[name] = Trk(shape, 4, ng)
        return t

    def psum(self, name, shape, dtype=F32, ng=4):
        t = self.st.enter_context(self.nc.psum_tensor(name, list(shape), dtype))
        self.trk[name] = Trk(shape, 4, ng)
        return t

    def dram(self, name, shape, dtype, kind="Internal", nq=8, ng=8):
        t = self.nc.dram_tensor(name, list(shape), dtype, kind=kind)
        self.trk[name] = Trk(shape, nq, ng)
        return t.ap()

    def _deps(self, reads, writes):
        deps = {}
        trk = self.trk
        for ap in reads:
            t = trk[ap.tensor.name]
            for c in t.cells(ap):
                w = t.w[c]
                if w is not None and deps.get(w[0], (None, 0))[1] < w[2]:
                    deps[w[0]] = (w[1], w[2])
        for ap in writes:
            t = trk[ap.tensor.name]
            for c in t.cells(ap):
                w = t.w[c]
                if w is not None and deps.get(w[0], (None, 0))[1] < w[2]:
                    deps[w[0]] = (w[1], w[2])
                for tok in t.r[c].values():
                    if deps.get(tok[0], (None, 0))[1] < tok[2]:
                        deps[tok[0]] = (tok[1], tok[2])
        return deps

    def _wait(self, eng, deps):
        for key, (sem, val) in deps.items():
            if key == eng.key and not eng.same_wait:
                continue
            if eng.seen.get(key, 0) < val:
                eng.e.wait_ge(sem, val)
                eng.seen[key] = val

    def _record(self, tok, reads, writes):
        trk = self.trk
        key = tok[0]
        for ap in reads:
            t = trk[ap.tensor.name]
            for c in t.cells(ap):
                t.r[c][key] = tok
        for ap in writes:
            t = trk[ap.tensor.name]
            for c in t.cells(ap):
                t.w[c] = tok
                t.r[c] = {}

    def op(self, eng, fn, reads, writes, inc=True):
        self._wait(eng, self._deps(reads, writes))
        ins = fn(eng.e)
        self.n_ins += 1
        if inc:
            eng.n += 1
            ins.then_inc(eng.sem, 1)
            tok = (eng.key, eng.sem, eng.n)
        else:
            tok = (eng.key, eng.sem, eng.n + 1)
        self._record(tok, reads, writes)
        if inc and eng.n >= 30000:
            eng.prev = (eng.key, eng.sem, eng.n)
            eng.gen = getattr(eng, "gen", 0) + 1
            eng.sem = self.st0.enter_context(self.nc.semaphore(f"s_{eng.name}_{eng.gen}"))
            eng.key = (eng.name, eng.gen)
            eng.n = 0
        return ins

    def dma(self, q, out, in_, **kw):
        deps = self._deps([in_], [out])
        self._wait(q, deps)
        i = q.slot_i % len(q.slots)
        q.slot_i += 1
        sem = q.slots[i]
        key = ("dma", q.name, i)
        prev = q.slot_val[i]
        if prev and q.seen.get(key, 0) < prev:
            q.e.wait_ge(sem, prev)
            q.seen[key] = prev
        q.e.dma_start(out=out, in_=in_, **kw).then_inc(sem, 16)
        self.n_ins += 1
        q.slot_val[i] = prev + 16
        tok = (key, sem, prev + 16)
        self._record(tok, [in_], [out])
        return tok

    def finish(self):
        for q in (self.sp, self.pool, self.act):
            for i, sem in enumerate(q.slots):
                if q.slot_val[i]:
                    self.sp.e.wait_ge(sem, q.slot_val[i])
        for e in (self.pe, self.dve, self.act, self.pool):
            if getattr(e, "prev", None):
                self.sp.e.wait_ge(e.prev[1], e.prev[2])
            if e.n:
                self.sp.e.wait_ge(e.sem, e.n)

    def mm(self, out, lhsT, rhs, start=True, stop=True, inc=None):
        if inc is None:
            inc = True
        return self.op(self.pe, lambda e: e.matmul(out, lhsT=lhsT, rhs=rhs, start=start, stop=stop),
                       [lhsT, rhs], [out], inc=inc)

    def tr(self, out, in_, ident):
        return self.op(self.pe, lambda e: e.transpose(out, in_, ident), [in_, ident], [out])

    def actv(self, out, in_, func, bias=None, scale=1.0, accum_out=None, eng=None):
        rd = [in_]
        kw = {}
        if bias is not None:
            kw["bias"] = bias
            if not isinstance(bias, (int, float)):
                rd.append(bias)
        if not isinstance(scale, (int, float)):
            rd.append(scale)
        wr = [out]
        if accum_out is not None:
            kw["accum_out"] = accum_out
            wr.append(accum_out)
        return self.op(self.act, lambda e: e.activation(out=out, in_=in_, func=func, scale=scale, **kw), rd, wr)

    def tt(self, eng, out, in0, in1, op):
        return self.op(eng, lambda e: e.tensor_tensor(out=out, in0=in0, in1=in1, op=op), [in0, in1], [out])

    def ts(self, eng, out, in0, s1, op0, s2=None, op1=None, accum_out=None):
        rd = [in0] + [s for s in (s1, s2) if s is not None and not isinstance(s, (int, float))]
        wr = [out] + ([accum_out] if accum_out is not None else [])
        kw = {}
        if op1 is not None:
            kw["op1"] = op1
        if accum_out is not None:
            kw["accum_out"] = accum_out
        return self.op(eng, lambda e: e.tensor_scalar(out=out, in0=in0, scalar1=s1, scalar2=s2, op0=op0, **kw), rd, wr)

    def stt(self, eng, out, in0, scalar, in1, op0, op1):
        rd = [in0, in1] + ([scalar] if not isinstance(scalar, (int, float)) else [])
        return self.op(eng, lambda e: e.scalar_tensor_tensor(out=out, in0=in0, scalar=scalar, in1=in1, op0=op0, op1=op1),
                       rd, [out])

    def copy(self, eng, out, in_):
        if eng is self.act:
            return self.op(eng, lambda e: e.copy(out=out, in_=in_), [in_], [out])
        return self.op(eng, lambda e: e.tensor_copy(out=out, in_=in_), [in_], [out])

    def memset(self, eng, ap, val):
        return self.op(eng, lambda e: e.memset(ap, val), [], [ap])


class _Scope:
    def __init__(self, k):
        self.k = k

    def __enter__(self):
        self.old = self.k.st
        self.new = ExitStack()
        self.new.__enter__()
        self.k.st = self.new
        return self

    def __exit__(self, *a):
        self.k.barrier()
        self.k.st = self.old
        return self.new.__exit__(*a)


def _k_scope(self):
    return _Scope(self)


def _k_barrier(self):
    engs = (self.pe, self.dve, self.act, self.pool, self.sp)
    for e in engs:
        for x in engs:
            pv = getattr(x, "prev", None)
            if x is not e and pv and e.seen.get(pv[0], 0) < pv[2]:
                e.e.wait_ge(pv[1], pv[2])
                e.seen[pv[0]] = pv[2]
            if x is not e and x.n and e.seen.get(x.key, 0) < x.n:
                e.e.wait_ge(x.sem, x.n)
                e.seen[x.key] = x.n
        for q in (self.sp, self.pool, self.act):
            for i, sem in enumerate(q.slots):
                key = ("dma", q.name, i)
                if q.slot_val[i] and e.seen.get(key, 0) < q.slot_val[i]:
                    e.e.wait_ge(sem, q.slot_val[i])
                    e.seen[key] = q.slot_val[i]


K.scope = _k_scope
K.barrier = _k_barrier
from concourse.bass_utils import run_bass_kernel_spmd
D = 2048
EPS = 1e-6
TC = 256
NG = 24


def build(T, stages="all"):
    nc = bass.Bass("TRN2", target_bir_lowering=False)
    st = ExitStack()
    st.__enter__()
    k = K(nc, st)
    dve, act, pool, pe, sp = k.dve, k.act, k.pool, k.pe, k.sp
    cnt = [0]

    def nm(p):
        cnt[0] += 1
        return f"{p}{cnt[0]}"

    def ein(name, shape, dt=F32):
        t = nc.dram_tensor(name, list(shape), dt, kind="ExternalInput").ap()
        k.trk[name] = Trk(shape, 8, 8)
        return t

    x = ein("x", [T, D])
    ctx = ein("ctx", [TC, D])
    cc = ein("cc", [128, 32])
    w_mod = ein("w_mod", [2, D, 12288])
    b_mod = ein("b_mod", [2, 12288])
    gains = ein("gains", [2, 4, D])
    w_in_ab = ein("w_in_ab", [D, 2560])
    pool_w = ein("pool_w", [1024, 256])
    pool_scale = ein("pool_scale", [128, 8])
    qk_norm = ein("qk_norm", [2, 128])
    w_out_ab = ein("w_out_ab", [2048, D])
    w_in_c = ein("w_in_c", [D, 12416])
    conv_c = ein("conv_c", [128, 64 * 5])
    adt = ein("adt", [2, 64])
    gnorm = ein("gnorm", [1, 128])
    w_out_c = ein("w_out_c", [4096, D])
    w_up = ein("w_up", [2, D, 8192])
    w_down = ein("w_down", [2, 8192, D])
    cos_l = ein("cos_l", [T, 64])
    sin_l = ein("sin_l", [T, 64])
    cos_c = ein("cos_c", [TC, 64])
    sin_c = ein("sin_c", [TC, 64])
    invc_l = ein("invc_l", [4, T])
    invc_c = ein("invc_c", [4, TC])
    masks = ein("masks", [128, 19 * 128])
    out = nc.dram_tensor("out", [T, D], F32, kind="ExternalOutput").ap()
    k.trk["out"] = Trk([T, D], 8, 8)

    mods_d = k.dram("mods_d", [2, 2, 12288], F32)
    wb_in_ab = k.dram("wb_in_ab", [D, 2560], BF16)
    wb_pool = k.dram("wb_pool", [1024, 256], BF16)
    wb_out_ab = k.dram("wb_out_ab", [2048, D], BF16)
    wb_in_c = k.dram("wb_in_c", [D, 12416], BF16)
    wb_out_c = k.dram("wb_out_c", [4096, D], BF16)
    wb_up = [k.dram(f"wb_up{l}", [D, 8192], BF16) for l in range(2)]
    wb_down = [k.dram(f"wb_down{l}", [8192, D], BF16) for l in range(2)]
    qT_l = k.dram("qT_l", [8, 128, T], BF16)
    qT_c = k.dram("qT_c", [8, 128, TC], BF16)
    uT_l = k.dram("uT_l", [1024, T + 16], F32)
    uT_c = k.dram("uT_c", [1024, TC + 16], F32)
    xm_l = k.dram("xm_l", [T, D], F32)
    xm_c = k.dram("xm_c", [TC, D], F32)
    x1_l = k.dram("x1_l", [T, D], F32)
    x1_c = k.dram("x1_c", [TC, D], F32)
    x2_l = k.dram("x2_l", [T, D], F32)

    pb = [k.psum(f"pb{i}", [128, 512], F32) for i in range(8)]

    ident = k.sbuf("ident", [128, 128], F32)
    identb = k.sbuf("identb", [128, 128], BF16)
    onesb = k.sbuf("onesb", [128, 128], BF16)
    k.dma(sp, ident[:], masks[:, 0:128])
    k.copy(dve, identb[:], ident[:])
    k.memset(dve, onesb[:], 1.0)

    def cast_w(dst, src, rows, cols):
        i = 0
        for r in range(0, rows, 128):
            for c0 in range(0, cols, 2048):
                cw_ = min(2048, cols - c0)
                f32t = cst32[i % 3]
                b16t = cst16[i % 3]
                k.dma(sp, f32t[:, 0:cw_], src[r:r + 128, c0:c0 + cw_])
                eng = (act, dve, pool)[i % 3]
                k.copy(eng, b16t[:, 0:cw_], f32t[:, 0:cw_])
                k.dma(sp, dst[r:r + 128, c0:c0 + cw_], b16t[:, 0:cw_])
                i += 1

    with k.scope():
        cst32 = [k.sbuf(f"cst32_{i}", [128, 2048], F32) for i in range(3)]
        cst16 = [k.sbuf(f"cst16_{i}", [128, 2048], BF16) for i in range(3)]
        if stages != "l1":
            cast_w(wb_in_ab, w_in_ab, D, 2560)
            cast_w(wb_pool, pool_w, 1024, 256)
            cast_w(wb_out_ab, w_out_ab, 2048, D)
            cast_w(wb_up[0], w_up[0], D, 8192)
            cast_w(wb_down[0], w_down[0], 8192, D)
        if stages != "l0":
            cast_w(wb_in_c, w_in_c, D, 12416)
            cast_w(wb_out_c, w_out_c, 4096, D)
            cast_w(wb_up[1], w_up[1], D, 8192)
            cast_w(wb_down[1], w_down[1], 8192, D)

    with k.scope():
        ccs = k.sbuf("ccs", [128, 32], F32)
        sc = k.sbuf("sc", [128, 32], F32)
        bm = k.sbuf("bm", [2, 12288], F32, ng=24)
        wm = [k.sbuf(f"wm{i}", [128, 16, 512], F32) for i in range(2)]
        k.dma(sp, ccs[:], cc[:])
        k.actv(sc[:], ccs[:], AF.Silu)
        for l in range(2):
            k.dma(sp, bm[0:1, :], b_mod[l:l + 1, :])
            k.dma(sp, bm[1:2, :], b_mod[l:l + 1, :])
            wv = w_mod[l].rearrange("(kk p) n -> p kk n", p=128)
            for n in range(NG):
                w_t = wm[n % 2]
                k.dma(sp, w_t[:], wv[:, :, n * 512:(n + 1) * 512])
                ps = pb[n % 2]
                for kk in range(16):
                    k.mm(ps[0:2, :], sc[:, kk:32:16], w_t[:, kk, :], start=(kk == 0), stop=(kk == 15), inc=(kk == 15))
                k.tt(dve, bm[:, n * 512:(n + 1) * 512], ps[0:2, :], bm[:, n * 512:(n + 1) * 512], ALU.add)
            k.dma(sp, mods_d[l], bm[:])

    def bload(dst, row_ap):
        k.dma(sp, dst, row_ap.partition_broadcast(128))

    def mod_vecs(l, seq, which, g):
        m = mods_d[l, seq]
        if which in ("pre0", "pre1"):
            si, ci, gi = (0, 1, 0) if which == "pre0" else (3, 4, 2)
            A = k.sbuf(nm("A"), [128, D], F32)
            B = k.sbuf(nm("B"), [128, D], F32)
            bload(A[:], m[ci * D:(ci + 1) * D])
            bload(g[:], gains[l, gi, :])
            bload(B[:], m[si * D:(si + 1) * D])
            k.stt(dve, A[:], A[:], 1.0, g[:], ALU.add, ALU.mult)
            return A, B
        gi_m, gi_g = (2, 1) if which == "g0" else (5, 3)
        G = k.sbuf(nm("G"), [128, D], F32)
        bload(G[:], m[gi_m * D:(gi_m + 1) * D])
        bload(g[:], gains[l, gi_g, :])
        k.tt(dve, G[:], G[:], g[:], ALU.mult)
        return G

    def rstd_of(ss, n, eps=EPS):
        k.ts(dve, ss, ss, 1.0 / n, ALU.mult, eps, ALU.add)
        k.actv(ss, ss, AF.Sqrt)
        k.op(dve, lambda e: e.reciprocal(out=ss, in_=ss), [ss], [ss])

    def norm_mod(xt, A, B, h, junk, ss):
        k.actv(junk[:], xt[:], AF.Square, accum_out=ss[:, 0:1])
        rstd_of(ss[:, 0:1], D)
        k.stt(dve, junk[:], xt[:], ss[:, 0:1], A[:], ALU.mult, ALU.mult)
        k.tt(pool, h[:], junk[:], B[:], ALU.add)

    def transpose_to(hT, h, ncol, tsl, pbs):
        for c4 in range(0, ncol, 4):
            ps = pbs[(c4 // 4) % len(pbs)]
            n4 = min(4, ncol - c4)
            for j in range(n4):
                k.tr(ps[:, j * 128:(j + 1) * 128], h[:, (c4 + j) * 128:(c4 + j + 1) * 128], ident[:])
            eng = act if (c4 // 4) % 2 == 0 else dve
            k.copy(eng, hT[:, c4:c4 + n4, tsl], ps[:, 0:n4 * 128].rearrange("p (c t) -> p c t", t=128))

    def residual_out(zps_list, xt, G, dst_rows, z, ss, sqj):
        for n in range(4):
            k.copy(act if n % 2 == 0 else dve, z[:, n * 512:(n + 1) * 512], zps_list[n][:])
        k.actv(sqj[:], z[:], AF.Square, accum_out=ss[:, 0:1])
        rstd_of(ss[:, 0:1], D)
        k.stt(dve, z[:], z[:], ss[:, 0:1], G[:], ALU.mult, ALU.mult)
        k.tt(pool, z[:], z[:], xt[:], ALU.add)
        k.dma(sp, dst_rows, z[:])

    NKT = (TC + T) // 128
    kT_s = k.dram("kT_s", [2, 128, TC + T], BF16)
    v_s = k.dram("v_s", [TC + T, 256], BF16)
    if stages != "l1":
        with k.scope():
            win = k.sbuf("win", [128, 16, 2560], BF16, ng=16)
            k.dma(sp, win[:], wb_in_ab.rearrange("(kk p) n -> p kk n", p=128))
            qg = k.sbuf("qg", [128, 128], F32)
            kg = k.sbuf("kg", [128, 128], F32)
            bload(qg[:], qk_norm[0, :])
            bload(kg[:], qk_norm[1, :])
            zt = k.sbuf("zt", [128, 8, 8], F32)
            k.memset(dve, zt[:], 0.0)
            for (uT, Tn) in ((uT_l, T), (uT_c, TC)):
                for off in (0, Tn + 8):
                    k.dma(sp, uT[:, off:off + 8].rearrange("(c p) t -> p c t", p=128), zt[:])
            xt_b = [k.sbuf(f"s0x{i}", [128, D], F32) for i in range(2)]
            junk = k.sbuf("s0junk", [128, D], F32)
            h = k.sbuf("s0h", [128, D], F32)
            hT = k.sbuf("s0hT", [128, 16, 128], BF16)
            ss = k.sbuf("s0ss", [128, 16], F32)
            qf = k.sbuf("s0qf", [128, 10, 128], F32)
            qr = k.sbuf("s0qr", [128, 10, 128], F32)
            t1 = k.sbuf("s0t1", [128, 10, 64], F32)
            t2 = k.sbuf("s0t2", [128, 10, 64], F32)
            cs = k.sbuf("s0cs", [128, 2, 64], F32)
            qTt = k.sbuf("s0qTt", [128, 8, 128], BF16)
            ut = k.sbuf("s0ut", [128, 8, 128], F32)
            kTt = k.sbuf("s0kTt", [128, 2, 128], BF16)
            vt = k.sbuf("s0vt", [128, 256], BF16)
            for (xin, Tn, seq, cos_t, sin_t, qT_s, uT_s, ko) in (
                    (ctx, TC, 1, cos_c, sin_c, qT_c, uT_c, 0), (x, T, 0, cos_l, sin_l, qT_l, uT_l, TC)):
                with k.scope():
                    A, B = mod_vecs(0, seq, "pre0", junk)
                    for ti in range(Tn // 128):
                        t0 = ti * 128
                        xt = xt_b[ti % 2]
                        k.dma(sp, xt[:], xin[t0:t0 + 128, :])
                        k.dma(sp, cs[:, 0, :], cos_t[t0:t0 + 128, :])
                        k.dma(sp, cs[:, 1, :], sin_t[t0:t0 + 128, :])
                        norm_mod(xt, A, B, h, junk, ss)
                        transpose_to(hT, h, 16, slice(0, 128), [pb[6], pb[7]])
                        for n in range(3):
                            for kk in range(16):
                                k.mm(pb[n][:], hT[:, kk, :], win[:, kk, 1024 + n * 512:1024 + (n + 1) * 512],
                                     start=(kk == 0), stop=(kk == 15), inc=(kk == 15))
                        k.copy(act, vt[:], pb[2][:, 256:512])
                        k.dma(sp, v_s[ko + t0:ko + t0 + 128, :], vt[:])
                        for n in range(2):
                            k.copy(act, qf[:, n * 4:(n + 1) * 4, :].rearrange("p h d -> p (h d)"), pb[n][:])
                        k.copy(act, qf[:, 8:10, :].rearrange("p h d -> p (h d)"), pb[2][:, 0:256])
                        k.tt(dve, qr[:], qf[:], qf[:], ALU.mult)
                        k.op(dve, lambda e: e.tensor_reduce(out=ss[:, 1:11], in_=qr[:], axis=AX.X, op=ALU.add),
                             [qr[:]], [ss[:, 1:11]])
                        rstd_of(ss[:, 1:11], 128)
                        k.tt(dve, qf[:], qf[:], ss[:, 1:11].unsqueeze(2).to_broadcast([128, 10, 128]), ALU.mult)
                        k.tt(dve, qf[:, 0:8, :], qf[:, 0:8, :], qg[:].unsqueeze(1).to_broadcast([128, 8, 128]), ALU.mult)
                        k.tt(dve, qf[:, 8:10, :], qf[:, 8:10, :], kg[:].unsqueeze(1).to_broadcast([128, 2, 128]), ALU.mult)
                        cb = cs[:, 0, :].unsqueeze(1).to_broadcast([128, 10, 64])
                        sb = cs[:, 1, :].unsqueeze(1).to_broadcast([128, 10, 64])
                        x1 = qf[:, :, 0:64]
                        x2 = qf[:, :, 64:128]
                        k.tt(dve, t1[:], x1, cb, ALU.mult)
                        k.tt(pool, t2[:], x2, sb, ALU.mult)
                        k.tt(dve, qr[:, :, 0:64], t1[:], t2[:], ALU.subtract)
                        k.tt(dve, t1[:], x1, sb, ALU.mult)
                        k.tt(pool, t2[:], x2, cb, ALU.mult)
                        k.tt(dve, qr[:, :, 64:128], t1[:], t2[:], ALU.add)
                        qrf = qr[:].rearrange("p h d -> p (h d)")
                        transpose_to(qTt, qrf, 8, slice(0, 128), [pb[3], pb[4]])
                        k.dma(sp, qT_s[:, :, t0:t0 + 128].rearrange("h d t -> d h t"), qTt[:])
                        transpose_to(kTt, qrf[:, 1024:1280], 2, slice(0, 128), [pb[5]])
                        k.dma(sp, kT_s[:, :, ko + t0:ko + t0 + 128].rearrange("h d t -> d h t"), kTt[:])
                        for c in range(8):
                            ps = pb[c % 2]
                            for kk in range(16):
                                k.mm(ps[:, 0:128], win[:, kk, c * 128:(c + 1) * 128], hT[:, kk, :],
                                     start=(kk == 0), stop=(kk == 15), inc=(kk == 15))
                            k.copy(act if c % 2 == 0 else dve, ut[:, c, :], ps[:, 0:128])
                        k.dma(sp, uT_s[:, 8 + t0:8 + t0 + 128].rearrange("(c p) t -> p c t", p=128), ut[:])

        with k.scope():
            kT_all = k.sbuf("kT_all", [128, 2, TC + T], BF16, ng=16)
            v_all = k.sbuf("v_all", [128, NKT, 256], BF16, ng=16)
            for j in range(2):
                k.dma(sp, kT_all[:, j, :], kT_s[j])
            k.dma(sp, v_all[:], v_s.rearrange("(kt p) c -> p kt c", p=128))
            wo_b = [k.sbuf(f"wo{i}", [128, 16, 512], BF16) for i in range(2)]
            wov = wb_out_ab.rearrange("(c p) n -> p c n", p=128)
            wpl = k.sbuf("wpl", [128, 8, 256], BF16)
            k.dma(sp, wpl[:], wb_pool.rearrange("(c p) n -> p c n", p=128))
            psc = k.sbuf("psc", [128, 8], F32)
            k.dma(sp, psc[:], pool_scale[:])
            catT = k.sbuf("catT", [128, 16, 512], BF16, ng=16)
            qTs = [k.sbuf(f"s1q{i}", [128, 512], BF16) for i in range(2)]
            pT = [k.sbuf(f"s1p{i}", [128, 512], BF16) for i in range(3)]
            rcp = k.sbuf("s1rcp", [128, 512], F32)
            uw = [k.sbuf(f"s1uw{i}", [128, 528], F32) for i in range(2)]
            sA = k.sbuf("s1sA", [128, 528], F32)
            sB = k.sbuf("s1sB", [128, 528], F32)
            icb = k.sbuf("s1icb", [128, 512], F32)
            dT = k.sbuf("s1dT", [128, 2, 512], BF16)
            xt_b = [k.sbuf(f"s1x{i}", [128, D], F32) for i in range(2)]
            junk = k.sbuf("s1junk", [128, D], F32)
            sqj = k.sbuf("s1sqj", [128, D], BF16)
            ss = k.sbuf("s1ss", [128, 4], F32)
            scale = 128.0 ** -0.5
            for (xin, Tn, seq, qT_s, uT_s, invc, kt0, nkt, xm) in (
                    (ctx, TC, 1, qT_c, uT_c, invc_c, 0, TC // 128, xm_c), (x, T, 0, qT_l, uT_l, invc_l, 0, NKT, xm_l)):
                with k.scope():
                    G = mod_vecs(0, seq, "g0", junk)
                    N = min(512, Tn)
                    for qi in range(Tn // N):
                        q0 = qi * N
                        for hh in range(8):
                            j = hh // 4
                            qt = qTs[hh % 2]
                            k.dma(sp, qt[:, 0:N], qT_s[hh, :, q0:q0 + N])
                            po, psm = pb[4 + (hh % 2) * 2], pb[5 + (hh % 2) * 2]
                            for kt in range(kt0, kt0 + nkt):
                                ps = pb[kt % 3]
                                k.mm(ps[:, 0:N], kT_all[:, j, kt * 128:(kt + 1) * 128], qt[:, 0:N])
                                p_t = pT[kt % 3]
                                k.actv(p_t[:, 0:N], ps[:, 0:N], AF.Exp, scale=scale)
                                first, last = kt == kt0, kt == kt0 + nkt - 1
                                k.mm(po[:, 0:N], v_all[:, kt, j * 128:(j + 1) * 128], p_t[:, 0:N], start=first, stop=last, inc=False)
                                k.mm(psm[:, 0:N], onesb[:], p_t[:, 0:N], start=first, stop=last)
                            k.op(dve, lambda e: e.reciprocal(out=rcp[:, 0:N], in_=psm[:, 0:N]), [psm[:, 0:N]], [rcp[:, 0:N]])
                            k.tt(dve, catT[:, 8 + hh, 0:N], po[:, 0:N], rcp[:, 0:N], ALU.mult)
                        for g, w in enumerate((2, 4, 8, 16)):
                            k.dma(sp, icb[:, 0:N], invc[g, q0:q0 + N].partition_broadcast(128))
                            for c2 in range(2):
                                c = g * 2 + c2
                                u = uw[c % 2]
                                k.dma(sp, u[:, 0:N + 16], uT_s[c * 128:(c + 1) * 128, q0:q0 + N + 16])
                                W = N + 16
                                k.tt(dve, sA[:, 1:W], u[:, 0:W - 1], u[:, 1:W], ALU.add)
                                cur, oth, lo, hi = sA, sB, 1, W
                                for sh in (1, 2, 4)[:g]:
                                    k.tt(dve if sh != 2 else pool, oth[:, lo + sh:hi - sh], cur[:, lo:hi - 2 * sh], cur[:, lo + 2 * sh:hi], ALU.add)
                                    cur, oth, lo, hi = oth, cur, lo + sh, hi - sh
                                k.tt(dve, oth[:, 8:8 + N], cur[:, 8:8 + N], icb[:, 0:N], ALU.mult)
                                k.tt(dve, dT[:, c2, 0:N], oth[:, 8:8 + N], u[:, 8:8 + N], ALU.subtract)
                            for d2 in range(2):
                                ps = pb[d2]
                                for c2 in range(2):
                                    k.mm(ps[:, 0:N], wpl[:, g * 2 + c2, d2 * 128:(d2 + 1) * 128], dT[:, c2, 0:N],
                                         start=(c2 == 0), stop=(c2 == 1))
                                k.ts(dve, catT[:, g * 2 + d2, 0:N], ps[:, 0:N], psc[:, g * 2 + d2:g * 2 + d2 + 1], ALU.mult)
                        for s in range(N // 128):
                            t0 = q0 + s * 128
                            xt = xt_b[s % 2]
                            k.dma(sp, xt[:], xin[t0:t0 + 128, :])
                            zp = [pb[n] for n in range(4)]
                            for n in range(4):
                                wo = wo_b[n % 2]
                                k.dma(sp, wo[:], wov[:, :, n * 512:(n + 1) * 512])
                                for c in range(16):
                                    k.mm(zp[n][:], catT[:, c, s * 128:(s + 1) * 128], wo[:, c, :],
                                         start=(c == 0), stop=(c == 15), inc=(c == 15))
                            residual_out(zp, xt, G, xm[t0:t0 + 128, :], junk, ss, sqj)

    def mlp(l, seq, xin, xout, Tn):
        with k.scope():
            junk = k.sbuf(nm("mj"), [128, D], F32)
            A, B = mod_vecs(l, seq, "pre1", junk)
            G = mod_vecs(l, seq, "g1", junk)
            N = min(512, Tn)
            NS = N // 128
            xt_b = [k.sbuf(nm("mx"), [128, D], F32) for i in range(2)]
            h = k.sbuf(nm("mh"), [128, D], F32)
            sqj = k.sbuf(nm("msq"), [128, D], BF16)
            ss = k.sbuf(nm("mss"), [128, 4], F32)
            hT = k.sbuf(nm("mhT"), [128, 16, 512], BF16, ng=16)
            aT = k.sbuf(nm("maT"), [128, 64, 512], BF16, ng=64)
            rl = k.sbuf(nm("mrl"), [128, 512], F32)
            wu = [k.sbuf(nm("mwu"), [128, 16, 256], BF16) for i in range(2)]
            wd = [k.sbuf(nm("mwd"), [128, D], BF16) for i in range(4)]
            wuv = wb_up[l].rearrange("(kk p) n -> p kk n", p=128)
            for qi in range(Tn // N):
                q0 = qi * N
                for s in range(NS):
                    k.dma(sp, xt_b[s % 2][:], xin[q0 + s * 128:q0 + (s + 1) * 128, :])
                    norm_mod(xt_b[s % 2], A, B, h, junk, ss)
                    transpose_to(hT, h, 16, slice(s * 128, (s + 1) * 128), [pb[6], pb[7]])
                for fb in range(32):
                    w_t = wu[fb % 2]
                    k.dma(sp, w_t[:], wuv[:, :, fb * 256:(fb + 1) * 256])
                    for f4 in range(2):
                        f = fb * 2 + f4
                        ps = pb[f % 4]
                        for kk in range(16):
                            k.mm(ps[:, 0:N], w_t[:, kk, f4 * 128:(f4 + 1) * 128], hT[:, kk, 0:N],
                                 start=(kk == 0), stop=(kk == 15), inc=(kk == 15))
                        k.actv(rl[:, 0:N], ps[:, 0:N], AF.Relu)
                        k.tt(dve, aT[:, f, 0:N], rl[:, 0:N], rl[:, 0:N], ALU.mult)
                for half in range(0, NS, 2):
                    ns = min(2, NS - half)
                    for f in range(64):
                        w_t = wd[f % 4]
                        k.dma(sp, w_t[:], wb_down[l][f * 128:(f + 1) * 128, :])
                        for s2 in range(ns):
                            s = half + s2
                            for n in range(4):
                                k.mm(pb[s2 * 4 + n][:], aT[:, f, s * 128:(s + 1) * 128], w_t[:, n * 512:(n + 1) * 512],
                                     start=(f == 0), stop=(f == 63), inc=(s2 == ns - 1 and n == 3))
                    for s2 in range(ns):
                        s = half + s2
                        t0 = q0 + s * 128
                        k.dma(sp, xt_b[s % 2][:], xin[t0:t0 + 128, :])
                        residual_out([pb[s2 * 4 + n] for n in range(4)], xt_b[s % 2], G, xout[t0:t0 + 128, :], junk, ss, sqj)

    if stages != "l1":
        mlp(0, 1, xm_c, x1_c, TC)
        mlp(0, 0, xm_l, x1_l if stages == "all" else out, T)


    if stages == "l0":
        k.finish()
        st.__exit__(None, None, None)
        return nc, k

    pT_s = {0: [k.dram(f"pT_l{i}", [4096, T + 4], F32, ng=16) for i in range(2)],
            1: [k.dram(f"pT_c{i}", [4096, TC + 4], F32, ng=16) for i in range(2)]}
    zs_s = k.dram("zs_l", [T, 4096], BF16)
    gb_s = {0: k.dram("gb_l", [T, 128], F32), 1: k.dram("gb_c", [TC, 128], F32)}
    qT1 = k.dram("qT1_l", [16, 128, T], BF16)
    kT1 = {0: k.dram("kT1_l", [16, 128, T], BF16), 1: k.dram("kT1_c", [16, 128, TC], BF16)}
    k1 = {0: k.dram("k1_l", [T, 16 * 128], BF16), 1: k.dram("k1_c", [TC, 16 * 128], BF16)}
    v1 = {0: k.dram("v1_l", [T, 32 * 128], BF16), 1: k.dram("v1_c", [TC, 32 * 128], BF16)}
    o_d = [k.dram(f"o_d{d}", [T, 4096], F32) for d in range(2)]
    seqs = ((1, x1_c, TC), (0, x1_l, T))

    with k.scope():
        junk = k.sbuf("d0junk", [128, D], F32)
        h = k.sbuf("d0h", [128, D], F32)
        xt_b = [k.sbuf(f"d0x{i}", [128, D], F32) for i in range(2)]
        ss = k.sbuf("d0ss", [128, 4], F32)
        hT = k.sbuf("d0hT", [128, 16, 2048], BF16, ng=16)
        wblk = [k.sbuf(f"d0w{i}", [128, 16, 512], BF16) for i in range(2)]
        stg = [k.sbuf(f"d0stg{i}", [128, 512], F32) for i in range(3)]
        zst = [k.sbuf(f"d0z{i}", [128, 512], BF16) for i in range(2)]
        gbt = k.sbuf("d0gb", [128, 128], F32)
        tmp64 = k.sbuf("d0t64", [128, 64], F32)
        adtb = k.sbuf("d0adt", [128, 2, 64], F32)
        zpad = k.sbuf("d0zp", [128, 64, 2], F32)
        bload(adtb[:, 0, :], adt[0, :])
        bload(adtb[:, 1, :], adt[1, :])
        k.actv(adtb[:, 0, :], adtb[:, 0, :], AF.Exp)
        k.memset(dve, zpad[:], 0.0)
        wcv = wb_in_c.rearrange("(kk p) n -> p kk n", p=128)
        for (seq, xin, Tn) in seqs:
            pT = pT_s[seq]
            for off in (0, Tn + 2):
                for i in range(2):
                    k.dma(sp, pT[i][:, off:off + 2].rearrange("(c p) t -> p c t", p=128), zpad[:, 0:32, :])
            with k.scope():
                A, B = mod_vecs(1, seq, "pre0", junk)
                for s0 in range(0, Tn, 2048):
                    ST = min(2048, Tn - s0)
                    for ti in range(ST // 128):
                        xt = xt_b[ti % 2]
                        k.dma(sp, xt[:], xin[s0 + ti * 128:s0 + (ti + 1) * 128, :])
                        norm_mod(xt, A, B, h, junk, ss)
                        transpose_to(hT, h, 16, slice(ti * 128, (ti + 1) * 128), [pb[6], pb[7]])
                    NB = 25 if seq == 0 else 25
                    for blk in range(25):
                        if seq == 1 and 16 <= blk < 24:
                            continue
                        ncol = 512 if blk < 24 else 128
                        w_t = wblk[blk % 2]
                        k.dma(sp, w_t[:, :, 0:ncol], wcv[:, :, blk * 512:blk * 512 + ncol])
                        if blk < 16:
                            for c4 in range(4):
                                c = blk * 4 + c4
                                for sub in range(0, ST, 512):
                                    N = min(512, ST - sub)
                                    ps = pb[(c4 + sub // 512) % 4]
                                    for kk in range(16):
                                        k.mm(ps[:, 0:N], w_t[:, kk, c4 * 128:(c4 + 1) * 128], hT[:, kk, sub:sub + N],
                                             start=(kk == 0), stop=(kk == 15), inc=(kk == 15))
                                    sg = stg[(c4 + sub // 512) % 3]
                                    k.copy(act if c4 % 2 == 0 else dve, sg[:, 0:N], ps[:, 0:N])
                                    k.dma(sp, pT[c // 32][(c % 32) * 128:(c % 32 + 1) * 128, 2 + s0 + sub:2 + s0 + sub + N], sg[:, 0:N])
                        elif blk < 24:
                            zb = blk - 16
                            for ti in range(ST // 128):
                                ps = pb[ti % 4]
                                for kk in range(16):
                                    k.mm(ps[:], hT[:, kk, ti * 128:(ti + 1) * 128], w_t[:, kk, :],
                                         start=(kk == 0), stop=(kk == 15), inc=(kk == 15))
                                zt_ = zst[ti % 2]
                                k.actv(zt_[:], ps[:], AF.Silu)
                                k.dma(sp, zs_s[s0 + ti * 128:s0 + (ti + 1) * 128, zb * 512:(zb + 1) * 512], zt_[:])
                        else:
                            for ti in range(ST // 128):
                                ps = pb[ti % 4]
                                for kk in range(16):
                                    k.mm(ps[:, 0:128], hT[:, kk, ti * 128:(ti + 1) * 128], w_t[:, kk, 0:128],
                                         start=(kk == 0), stop=(kk == 15), inc=(kk == 15))
                                k.actv(gbt[:, 0:64], ps[:, 0:64], AF.Sigmoid)
                                k.tt(dve, tmp64[:], ps[:, 64:128], adtb[:, 1, :], ALU.add)
                                k.actv(tmp64[:], tmp64[:], AF.Exp)
                                k.actv(tmp64[:], tmp64[:], AF.Ln, bias=1.0)
                                k.stt(dve, gbt[:, 64:128], tmp64[:], -1.0, adtb[:, 0, :], ALU.mult, ALU.mult)
                                k.dma(sp, gb_s[seq][s0 + ti * 128:s0 + (ti + 1) * 128, :], gbt[:])

    with k.scope():
        cw = k.sbuf("d1cw", [128, 320], F32)
        k.dma(sp, cw[:], conv_c[:])
        pw = [k.sbuf(f"d1pw{i}", [128, 516], F32) for i in range(2)]
        acc = k.sbuf("d1acc", [128, 512], F32)
        sv = k.sbuf("d1sv", [128, 512], F32)
        sqb = k.sbuf("d1sqb", [128, 512], BF16)
        rn = k.sbuf("d1rn", [128, 512], F32)
        xn = k.sbuf("d1xn", [128, 512], F32)
        xnb = [k.sbuf(f"d1xnb{i}", [128, 512], BF16) for i in range(2)]
        tok = [k.sbuf(f"d1tok{i}", [128, 4, 128], BF16) for i in range(2)]
        for (seq, xin, Tn) in seqs:
            pT = pT_s[seq]
            N = min(512, Tn)
            for q0 in range(0, Tn, N):
                for c in range(64):
                    if seq == 1 and c < 16:
                        continue
                    p_t = pw[c % 2]
                    k.dma(sp, p_t[:, 0:N + 4], pT[c // 32][(c % 32) * 128:(c % 32 + 1) * 128, q0:q0 + N + 4])
                    k.ts(dve, acc[:, 0:N], p_t[:, 0:N], cw[:, c * 5:c * 5 + 1], ALU.mult)
                    for j in range(1, 5):
                        k.stt(dve, acc[:, 0:N], p_t[:, j:j + N], cw[:, c * 5 + j:c * 5 + j + 1], acc[:, 0:N], ALU.mult, ALU.add)
                    k.actv(sv[:, 0:N], acc[:, 0:N], AF.Silu)
                    src = sv
                    if c < 32:
                        k.tt(pool, sqb[:, 0:N], sv[:, 0:N], sv[:, 0:N], ALU.mult)
                        ps = pb[c % 2]
                        k.mm(ps[:, 0:N], onesb[:], sqb[:, 0:N])
                        k.ts(dve, rn[:, 0:N], ps[:, 0:N], EPS, ALU.add)
                        k.actv(rn[:, 0:N], rn[:, 0:N], AF.Sqrt)
                        k.op(dve, lambda e: e.reciprocal(out=rn[:, 0:N], in_=rn[:, 0:N]), [rn[:, 0:N]], [rn[:, 0:N]])
                        if c < 16:
                            k.stt(dve, xn[:, 0:N], sv[:, 0:N], 128.0 ** -0.5, rn[:, 0:N], ALU.mult, ALU.mult)
                        else:
                            k.tt(dve, xn[:, 0:N], sv[:, 0:N], rn[:, 0:N], ALU.mult)
                        xb = xnb[c % 2]
                        k.copy(act, xb[:, 0:N], xn[:, 0:N])
                        if c < 16:
                            k.dma(sp, qT1[c, :, q0:q0 + N], xb[:, 0:N])
                            continue
                        k.dma(sp, kT1[seq][c - 16, :, q0:q0 + N], xb[:, 0:N])
                        src = xn
                    tk = tok[c % 2]
                    ps = pb[2 + c % 2]
                    for j in range(N // 128):
                        k.tr(ps[:, j * 128:(j + 1) * 128], src[:, j * 128:(j + 1) * 128], ident[:])
                    k.copy(act if c % 2 == 0 else dve, tk[:, 0:N // 128, :], ps[:, 0:N].rearrange("p (a b) -> p a b", b=128))
                    if c < 32:
                        dst = k1[seq][q0:q0 + N, (c - 16) * 128:(c - 15) * 128]
                    else:
                        dst = v1[seq][q0:q0 + N, (c - 32) * 128:(c - 31) * 128]
                    k.dma(sp, dst.rearrange("(a p) d -> p a d", p=128), tk[:, 0:N // 128, :])

    with k.scope():
        mk = k.sbuf("d2mk", [128, 4, 128], F32)
        k.dma(sp, mk[:].rearrange("p a b -> p (a b)"), masks[:, 128:640])
        ones32 = k.sbuf("d2ones", [128, 128], F32)
        k.memset(dve, ones32[:], 1.0)
        S = k.sbuf("d2S", [128, 32, 128], F32, ng=32)
        Sb = k.sbuf("d2Sb", [128, 32, 128], BF16, ng=32)
        gbt = k.sbuf("d2gb", [128, 128], F32)
        kTc = k.sbuf("d2kT", [128, 16, 128], BF16, ng=16)
        qTc = k.sbuf("d2qT", [128, 16, 128], BF16, ng=16)
        ktok = k.sbuf("d2kt", [128, 16, 128], BF16, ng=16)
        vtok = k.sbuf("d2vt", [128, 32, 128], BF16, ng=32)
        kd = k.sbuf("d2kd", [128, 32, 128], BF16, ng=32)
        gc = k.sbuf("d2gc", [128, 32], F32)
        gt = k.sbuf("d2gt", [128, 32], F32)
        eg = k.sbuf("d2eg", [128, 32], F32)
        ed = k.sbuf("d2ed", [128, 32], F32)
        egt = k.sbuf("d2egt", [128, 32], F32)
        Rd = k.sbuf("d2Rd", [128, 8, 128], F32)
        E = k.sbuf("d2E", [128, 8, 128], F32)
        tL = k.sbuf("d2tL", [128, 8, 128], F32)
        N32 = k.sbuf("d2N32", [128, 8, 128], F32)
        A32 = k.sbuf("d2A32", [128, 8, 128], F32)
        P = [k.sbuf(f"d2P{i}", [128, 8, 128], BF16) for i in range(2)]
        Pt = [k.sbuf(f"d2Pt{i}", [128, 8, 128], BF16) for i in range(2)]
        Tt = [k.sbuf(f"d2Tt{i}", [128, 8, 128], BF16) for i in range(2)]
        AtT = k.sbuf("d2AtT", [128, 8, 128], BF16)
        Tn_ = [k.sbuf(f"d2Tn{i}", [128, 8, 128], BF16) for i in range(2)]
        CbL = [k.sbuf(f"d2Cb{i}", [128, 8, 128], BF16) for i in range(7)]
        CtbL = [k.sbuf(f"d2Ctb{i}", [128, 8, 128], BF16) for i in range(7)]
        Wb = k.sbuf("d2Wb", [128, 8, 128], BF16)
        W2b = k.sbuf("d2W2b", [128, 8, 128], BF16)
        mq32 = k.sbuf("d2mq32", [128, 14, 128], F32, ng=14)
        k.dma(sp, mq32[:].rearrange("p a b -> p (a b)"), masks[:, 640:640 + 14 * 128])
        mLb = k.sbuf("d2mLb", [128, 7, 128], BF16)
        mUb = k.sbuf("d2mUb", [128, 7, 128], BF16)
        k.copy(dve, mLb[:], mq32[:, 0:7, :])
        k.copy(dve, mUb[:], mq32[:, 7:14, :])
        X32 = k.sbuf("d2X32", [128, 8, 128], F32)
        Xb = k.sbuf("d2Xb", [128, 8, 128], BF16)
        vn = k.sbuf("d2vn", [128, 8, 128], BF16)
        ot = k.sbuf("d2ot", [128, 32, 128], F32, ng=32)

        def v3(ps2):
            return None

        def bank8(i):
            return [pb[i + hh // 4][:, (hh % 4) * 128:(hh % 4 + 1) * 128] for hh in range(8)]

        def b3(i, half):
            return pb[i + half][:].rearrange("p (a b) -> p a b", b=128)

        for d in range(2):
            k.memset(dve, S[:], 0.0)
            k.memset(pool, Sb[:], 0.0)
            m_incl = mk[:, d, :]
            m_strict = mk[:, 2 + d, :]
            m_cumT = mk[:, 1 - d, :]
            order = []
            cc_ = list(range(TC // 128))
            lc_ = list(range(T // 128))
            if d == 1:
                cc_, lc_ = cc_[::-1], lc_[::-1]
            order = [(1, c) for c in cc_] + [(0, c) for c in lc_]
            for (seq, ci) in order:
                t0 = ci * 128
                want_o = seq == 0
                k.dma(sp, gbt[:], gb_s[seq][t0:t0 + 128, :])
                k.dma(sp, kTc[:], kT1[seq][:, :, t0:t0 + 128].rearrange("h d t -> d h t"))
                k.dma(sp, ktok[:].rearrange("p h d -> p (h d)"), k1[seq][t0:t0 + 128, :])
                k.dma(sp, vtok[:].rearrange("p h d -> p (h d)"), v1[seq][t0:t0 + 128, :])
                if want_o:
                    k.dma(sp, qTc[:], qT1[:, :, t0:t0 + 128].rearrange("h d t -> d h t"))
                beta = gbt[:, d * 32:(d + 1) * 32]
                g = gbt[:, 64 + d * 32:64 + (d + 1) * 32]
                k.mm(pb[0][:, 0:32], m_cumT, g)
                k.mm(pb[0][:, 32:64], ones32[:], g)
                k.copy(dve, gc[:], pb[0][:, 0:32])
                k.copy(dve, gt[:], pb[0][:, 32:64])
                k.actv(eg[:], gc[:], AF.Exp)
                k.actv(egt[:], gt[:], AF.Exp)
                k.tt(dve, ed[:], gt[:], gc[:], ALU.subtract)
                k.actv(ed[:], ed[:], AF.Exp)
                k.tt(dve, kd[:].rearrange("p (a b) d -> p a b d", b=2),
                     ktok[:].unsqueeze(2).to_broadcast([128, 16, 2, 128]),
                     ed[:].rearrange("p (a b) -> p a b", b=2).unsqueeze(3).to_broadcast([128, 16, 2, 128]), ALU.mult)
                for hg in range(4):
                    hs = slice(hg * 8, (hg + 1) * 8)
                    gcb = gc[:, hs].unsqueeze(2).to_broadcast([128, 8, 128])
                    k.tt(dve, Rd[:], gcb, ident[:].unsqueeze(1).to_broadcast([128, 8, 128]), ALU.mult)
                    for half in range(2):
                        k.mm(pb[half][:], ones32[:], Rd[:, half * 4:(half + 1) * 4, :].rearrange("p a b -> p (a b)"))
                    for half in range(2):
                        k.tt(dve, E[:, half * 4:(half + 1) * 4, :], gc[:, hg * 8 + half * 4:hg * 8 + half * 4 + 4].unsqueeze(2).to_broadcast([128, 4, 128]),
                             b3(0, half), ALU.subtract)
                    k.ts(pool, E[:], E[:], 0.0, ALU.min)
                    k.actv(E[:], E[:], AF.Exp)
                    k.tt(dve, E[:], E[:], m_incl.unsqueeze(1).to_broadcast([128, 8, 128]), ALU.mult)
                    for a in range(4):
                        hk = hg * 4 + a
                        k.mm(pb[2][:, a * 128:(a + 1) * 128], kTc[:, hk, :], kTc[:, hk, :])
                    if want_o:
                        for a in range(4):
                            hk = hg * 4 + a
                            k.mm(pb[3][:, a * 128:(a + 1) * 128], qTc[:, hk, :], kTc[:, hk, :])
                    E4 = E[:].rearrange("p (a b) j -> p a b j", b=2)
                    KKb = pb[2][:].rearrange("p (a j) -> p a j", j=128).unsqueeze(2).to_broadcast([128, 4, 2, 128])
                    k.tt(dve, tL[:].rearrange("p (a b) j -> p a b j", b=2), E4, KKb, ALU.mult)
                    k.tt(pool, tL[:], tL[:], m_strict.unsqueeze(1).to_broadcast([128, 8, 128]), ALU.mult)
                    k.stt(dve, N32[:], tL[:], -1.0, beta[:, hs].unsqueeze(2).to_broadcast([128, 8, 128]), ALU.mult, ALU.mult)
                    k.copy(act, P[0][:], N32[:])
                    if want_o:
                        QKb = pb[3][:].rearrange("p (a j) -> p a j", j=128).unsqueeze(2).to_broadcast([128, 4, 2, 128])
                        k.tt(dve, A32[:].rearrange("p (a b) j -> p a b j", b=2), E4, QKb, ALU.mult)
                    for hh in range(8):
                        k.tr(bank8(4)[hh], N32[:, hh, :], ident[:])
                    for half in range(2):
                        k.copy(act if half == 0 else dve, Pt[0][:, half * 4:(half + 1) * 4, :], b3(4, half))
                    if want_o:
                        for hh in range(8):
                            k.tr(bank8(6)[hh], A32[:, hh, :], ident[:])
                        for half in range(2):
                            k.copy(act if half == 0 else dve, AtT[:, half * 4:(half + 1) * 4, :], b3(6, half))
                    bc8 = lambda m: m.unsqueeze(1).to_broadcast([128, 8, 128])
                    mA = mLb if d == 0 else mUb
                    mB = mUb if d == 0 else mLb
                    cur = 0
                    for lv in range(7):
                        k.tt(pool, CbL[lv][:], P[0][:], bc8(mA[:, lv, :]), ALU.mult)
                        k.tt(pool, CtbL[lv][:], Pt[0][:], bc8(mB[:, lv, :]), ALU.mult)
                    k.tt(dve, Tn_[cur][:], CbL[0][:], bc8(identb[:]), ALU.add)
                    k.tt(dve, Tt[cur][:], CtbL[0][:], bc8(identb[:]), ALU.add)
                    for lv in range(1, 7):
                        nxt = 1 - cur
                        for hh in range(8):
                            k.mm(bank8(0)[hh], CbL[lv][:, hh, :], Tt[cur][:, hh, :], inc=(hh % 4 == 3))
                        for hh in range(8):
                            k.mm(bank8(2)[hh], CtbL[lv][:, hh, :], Tn_[cur][:, hh, :], inc=(hh % 4 == 3))
                        for half in range(2):
                            k.copy(act if half == 0 else dve, Wb[:, half * 4:(half + 1) * 4, :], b3(0, half))
                        for half in range(2):
                            k.copy(dve if half == 0 else act, W2b[:, half * 4:(half + 1) * 4, :], b3(2, half))
                        for hh in range(8):
                            k.mm(bank8(4)[hh], identb[:], Tt[cur][:, hh, :], start=True, stop=False, inc=False)
                            k.mm(bank8(4)[hh], Tn_[cur][:, hh, :], Wb[:, hh, :], start=False, stop=True, inc=(hh % 4 == 3))
                        for hh in range(8):
                            k.mm(bank8(6)[hh], identb[:], Tn_[cur][:, hh, :], start=True, stop=False, inc=False)
                            k.mm(bank8(6)[hh], Tt[cur][:, hh, :], W2b[:, hh, :], start=False, stop=True, inc=(hh % 4 == 3))
                        for half in range(2):
                            k.copy(act if half == 0 else dve, Tt[nxt][:, half * 4:(half + 1) * 4, :], b3(4, half))
                        for half in range(2):
                            k.copy(dve if half == 0 else act, Tn_[nxt][:, half * 4:(half + 1) * 4, :], b3(6, half))
                        cur = nxt
                    TT = Tt[cur]
                    for hh in range(8):
                        hv = hg * 8 + hh
                        k.mm(bank8(0)[hh], kTc[:, hv // 2, :], Sb[:, hv, :], inc=(hh % 4 == 3))
                    egb = eg[:, hs].unsqueeze(2).to_broadcast([128, 8, 128])
                    for half in range(2):
                        k.tt(dve, X32[:, half * 4:(half + 1) * 4, :], b3(0, half),
                             eg[:, hg * 8 + half * 4:hg * 8 + half * 4 + 4].unsqueeze(2).to_broadcast([128, 4, 128]), ALU.mult)
                    k.tt(pool, X32[:], vtok[:, hs, :], X32[:], ALU.subtract)
                    k.tt(dve, Xb[:], X32[:], beta[:, hs].unsqueeze(2).to_broadcast([128, 8, 128]), ALU.mult)
                    for hh in range(8):
                        k.mm(bank8(2)[hh], TT[:, hh, :], Xb[:, hh, :], inc=(hh % 4 == 3))
                    for half in range(2):
                        k.copy(act if half == 0 else dve, vn[:, half * 4:(half + 1) * 4, :], b3(2, half))
                    if want_o:
                        for hh in range(8):
                            hv = hg * 8 + hh
                            k.mm(bank8(4)[hh], qTc[:, hv // 2, :], Sb[:, hv, :], inc=(hh % 4 == 3))
                        for hh in range(8):
                            k.mm(bank8(6)[hh], AtT[:, hh, :], vn[:, hh, :], inc=(hh % 4 == 3))
                        for half in range(2):
                            osl = ot[:, hg * 8 + half * 4:hg * 8 + half * 4 + 4, :]
                            k.tt(dve, osl, b3(4, half),
                                 eg[:, hg * 8 + half * 4:hg * 8 + half * 4 + 4].unsqueeze(2).to_broadcast([128, 4, 128]), ALU.mult)
                            k.tt(dve, osl, osl, b3(6, half), ALU.add)
                    for hh in range(8):
                        hv = hg * 8 + hh
                        k.mm(bank8(0)[hh], kd[:, hv, :], vn[:, hh, :], inc=(hh % 4 == 3))
                    k.tt(pool, S[:, hs, :], S[:, hs, :], egt[:, hs].unsqueeze(2).to_broadcast([128, 8, 128]), ALU.mult)
                    for half in range(2):
                        ssl = S[:, hg * 8 + half * 4:hg * 8 + half * 4 + 4, :]
                        k.tt(dve, ssl, ssl, b3(0, half), ALU.add)
                    k.copy(act, Sb[:, hs, :], S[:, hs, :])
                if want_o:
                    k.dma(sp, o_d[d][t0:t0 + 128, :], ot[:].rearrange("p h d -> p (h d)"))

    with k.scope():
        junk = k.sbuf("d3junk", [128, D], F32)
        G = mod_vecs(1, 0, "g0", junk)
        gnb = k.sbuf("d3gn", [128, 128], F32)
        bload(gnb[:], gnorm[0, :])
        o0 = k.sbuf("d3o0", [128, 32, 128], F32, ng=32)
        o1 = k.sbuf("d3o1", [128, 32, 128], F32, ng=32)
        zt = k.sbuf("d3z", [128, 32, 128], BF16, ng=32)
        yT = k.sbuf("d3yT", [128, 32, 128], BF16, ng=32)
        wo = [k.sbuf(f"d3wo{i}", [128, 32, 512], BF16, ng=32) for i in range(2)]
        xt = k.sbuf("d3x", [128, D], F32)
        sqj = k.sbuf("d3sq", [128, D], BF16)
        ss = k.sbuf("d3ss", [128, 40], F32)
        wov = wb_out_c.rearrange("(c p) n -> p c n", p=128)
        for ti in range(T // 128):
            t0 = ti * 128
            k.dma(sp, o0[:].rearrange("p h d -> p (h d)"), o_d[0][t0:t0 + 128, :])
            k.dma(sp, o1[:].rearrange("p h d -> p (h d)"), o_d[1][t0:t0 + 128, :])
            k.dma(sp, zt[:].rearrange("p h d -> p (h d)"), zs_s[t0:t0 + 128, :])
            k.dma(sp, xt[:], x1_l[t0:t0 + 128, :])
            k.tt(dve, o0[:], o0[:], o1[:], ALU.add)
            k.tt(pool, o1[:], o0[:], o0[:], ALU.mult)
            k.op(dve, lambda e: e.tensor_reduce(out=ss[:, 4:36], in_=o1[:], axis=AX.X, op=ALU.add), [o1[:]], [ss[:, 4:36]])
            rstd_of(ss[:, 4:36], 128)
            k.tt(dve, o0[:], o0[:], ss[:, 4:36].unsqueeze(2).to_broadcast([128, 32, 128]), ALU.mult)
            k.tt(pool, o0[:], o0[:], gnb[:].unsqueeze(1).to_broadcast([128, 32, 128]), ALU.mult)
            k.tt(dve, o0[:], o0[:], zt[:], ALU.mult)
            transpose_to(yT, o0[:].rearrange("p h d -> p (h d)"), 32, slice(0, 128), [pb[4], pb[5], pb[6], pb[7]])
            zp = [pb[n] for n in range(4)]
            for n in range(4):
                w_t = wo[n % 2]
                k.dma(sp, w_t[:], wov[:, :, n * 512:(n + 1) * 512])
                for c in range(32):
                    k.mm(zp[n][:], yT[:, c, :], w_t[:, c, :], start=(c == 0), stop=(c == 31), inc=(c == 31))
            residual_out(zp, xt, G, x2_l[t0:t0 + 128, :], junk, ss, sqj)

    mlp(1, 0, x2_l, out, T)

    k.finish()
    st.__exit__(None, None, None)
    return nc, k


def _tables(T):
    t = np.arange(T)
    row = (t // 64).astype(np.float32)
    col = (t % 64).astype(np.float32)
    inv = (10000.0 ** (-np.arange(32, dtype=np.float32) / 32)).astype(np.float32)
    ang = np.concatenate([row[:, None] * inv, col[:, None] * inv], axis=-1).astype(np.float32)
    return np.cos(ang).astype(np.float32), np.sin(ang).astype(np.float32)


def _invc(L):
    t = np.arange(L)
    rows = []
    for w in (2, 4, 8, 16):
        lo = np.clip(t - w // 2, 0, L)
        hi = np.clip(t + w - w // 2, 0, L)
        rows.append(1.0 / (hi - lo).astype(np.float32))
    return np.stack(rows).astype(np.float32)


def _masks():
    i = np.arange(128)[:, None]
    j = np.arange(128)[None, :]
    ms = [i == j, j <= i, j >= i, j < i, j > i]
    quad = []
    for b in (1, 2, 4, 8, 16, 32, 64):
        quad.append((i // (2 * b) == j // (2 * b)) & (i % (2 * b) >= b) & (j % (2 * b) < b))
    ms = ms + quad + [q.T for q in quad]
    return np.ascontiguousarray(np.concatenate([m.astype(np.float32) for m in ms], axis=1))


def make_in_map(inp, b, T):
    f = lambda a: np.ascontiguousarray(np.asarray(a, dtype=np.float32))
    cos_l, sin_l = _tables(T)
    cc = np.concatenate([f(inp["c"][b]).reshape(16, 128).T, f(inp["c_ctx"]).reshape(16, 128).T], axis=1)
    return {
        "x": f(inp["x"][b][:T]), "ctx": f(inp["ctx"][b]), "cc": f(cc),
        "w_mod": f(inp["w_mod"]), "b_mod": f(inp["b_mod"]), "gains": f(inp["norm_gains"]),
        "w_in_ab": f(inp["w_in_ab"][0]), "pool_w": f(inp["pool_w"][0]).reshape(1024, 256),
        "pool_scale": f(f(inp["pool_scale"][0]).reshape(8, 128).T),
        "qk_norm": f(np.stack([inp["q_norm"][0], inp["k_norm"][0]])),
        "w_out_ab": f(inp["w_out_ab"][0]), "w_in_c": f(inp["w_in_c"][0]),
        "conv_c": f(f(inp["conv_c"][0]).reshape(5, 64, 128).transpose(2, 1, 0).reshape(128, 320)),
        "adt": f(np.stack([f(inp["a_log_c"][0]).reshape(64), f(inp["dt_bias_c"][0]).reshape(64)])),
        "gnorm": f(inp["gnorm_c"][0]).reshape(1, 128), "w_out_c": f(inp["w_out_c"][0]),
        "w_up": f(inp["w_up"]), "w_down": f(inp["w_down"]),
        "cos_l": cos_l, "sin_l": sin_l,
        "cos_c": np.ones((TC, 64), np.float32), "sin_c": np.zeros((TC, 64), np.float32),
        "invc_l": _invc(T), "invc_c": _invc(TC), "masks": _masks(),
    }


_CACHE = {}


def kernel(**inputs):
    T = inputs["x"].shape[1]
    B = inputs["x"].shape[0]
    if T not in _CACHE:
        _CACHE[T] = build(T)
    nc, _ = _CACHE[T]
    n = B
    maps = [make_in_map(inputs, c % B, T) for c in range(n)]
    res = run_bass_kernel_spmd(nc, maps, core_ids=list(range(n)))
    return np.stack([res.results[b]["out"] for b in range(B)]).astype(np.float32)
```
